# Optimizing a Trainium2 kernel written in Bass

```python
import jax, jax.numpy as jnp
from jax import lax
import numpy as np

D_MODEL = 1024
BATCH = 8
SEQ = 2048
DEPTH = 4

BRANCH_WIDTH = D_MODEL // 2
N_BRANCH = 3
HGRN_KEY = 128
HGRN_VAL = 128
HGRN_HEADS = BRANCH_WIDTH // HGRN_KEY
HGRN_CHUNK = 64
LB_FLOOR = 1e-30
CONV_K = 3
SG_CHUNK = 128
SG_GROUPS = 4
SG_GROUP_DIM = BRANCH_WIDTH // SG_GROUPS
D_FF = 4 * D_MODEL
NORM_EPS = 1e-6
LN_EPS = 1e-5
IN_COLS = 9 * BRANCH_WIDTH + N_BRANCH * D_MODEL

kernel_name = 'hybrid_hgrn2_shortconv_spatialgate_block'


def rms_norm(x, g, eps=NORM_EPS):
    xf = x.astype(jnp.float32)
    y = xf * lax.rsqrt(jnp.mean(xf * xf, axis=-1, keepdims=True) + eps)
    return (y * g.astype(jnp.float32)).astype(x.dtype)


def layer_norm(x, g, b, eps=LN_EPS):
    xf = x.astype(jnp.float32)
    mu = jnp.mean(xf, axis=-1, keepdims=True)
    xc = xf - mu
    y = xc * lax.rsqrt(jnp.mean(xc * xc, axis=-1, keepdims=True) + eps)
    return (y * g.astype(jnp.float32) + b.astype(jnp.float32)).astype(x.dtype)


def hgrn2_mix(q, fp, iv, go, lb, g_out):
    B, S, _ = q.shape
    H, K, V, L = HGRN_HEADS, HGRN_KEY, HGRN_VAL, HGRN_CHUNK
    N = S // L
    f32 = jnp.float32
    q = jax.nn.silu(q.astype(f32))
    fp = fp.astype(f32)
    lb = lb.astype(f32)
    logf = jnp.logaddexp(jnp.log(jnp.maximum(lb, LB_FLOOR)),
                         jnp.log1p(-lb) + jax.nn.log_sigmoid(fp))
    k = (1.0 - lb) * jax.nn.sigmoid(-fp)
    v = iv.astype(f32)

    def to_chunks(t, d):
        return t.reshape(B, N, L, H, d).transpose(1, 0, 3, 2, 4)

    qs, ks, vs, ls = to_chunks(q, K), to_chunks(k, K), to_chunks(v, V), to_chunks(logf, K)
    causal = jnp.tril(jnp.ones((L, L), dtype=bool))[:, :, None]

    def step(state, inp):
        qc, kc, vc, lc = inp
        b = jnp.cumsum(lc, axis=2)
        diff = b[:, :, :, None, :] - b[:, :, None, :, :]
        decay = jnp.where(causal, jnp.exp(jnp.where(causal, diff, 0.0)), 0.0)
        attn = jnp.einsum('bhtk,bhsk,bhtsk->bhts', qc, kc, decay)
        o = (jnp.einsum('bhts,bhsv->bhtv', attn, vc)
             + jnp.einsum('bhtk,bhkv->bhtv', qc * jnp.exp(b), state))
        b_last = b[:, :, -1:, :]
        new_state = (jnp.exp(b_last[:, :, 0, :])[..., None] * state
                     + jnp.einsum('bhsk,bhsv->bhkv', kc * jnp.exp(b_last - b), vc))
        return new_state, o

    state0 = jnp.zeros((B, H, K, V), f32)
    _, o = lax.scan(step, state0, (qs, ks, vs, ls))
    o = o.transpose(1, 0, 3, 2, 4).reshape(B, S, H, V)
    o = rms_norm(o, g_out.reshape(H, V)) * jax.nn.sigmoid(go.astype(f32)).reshape(B, S, H, V)
    return o.reshape(B, S, H * V).astype(iv.dtype)


def short_conv_mix(bg, cg, xc, w_conv):
    z = cg * xc
    ch = z.shape[-1]
    y = lax.conv_general_dilated(z, w_conv[:, None, :].astype(z.dtype), window_strides=(1,),
                                 padding=[(CONV_K - 1, 0)],
                                 dimension_numbers=('NWC', 'WIO', 'NWC'),
                                 feature_group_count=ch)
    return bg * y


def spatial_gating_mix(u, v, ln_g, ln_b, w_s, b_s):
    B, S, _ = u.shape
    N = S // SG_CHUNK
    u = jax.nn.gelu(u)
    v = layer_norm(jax.nn.gelu(v), ln_g, ln_b)
    vg = v.reshape(B, N, SG_CHUNK, SG_GROUPS, SG_GROUP_DIM)
    mask = jnp.tril(jnp.ones((SG_CHUNK, SG_CHUNK), dtype=w_s.dtype))
    sv = jnp.einsum('gts,bnsgd->bntgd', w_s * mask, vg) + b_s.T[:, :, None]
    return u * sv.reshape(B, S, BRANCH_WIDTH)


def setup_inputs(seed: int = 0) -> dict:
    key = jax.random.key(seed)
    ks = jax.random.split(key, 17)
    W, D = BRANCH_WIDTH, D_MODEL
    nrm = jax.random.normal
    f32 = jnp.float32
    return {
        'x': nrm(ks[0], (BATCH, SEQ, D), f32),
        'w_in': nrm(ks[1], (DEPTH, D, IN_COLS), f32) * D ** -0.5,
        'g_mix': 1.0 + 0.01 * nrm(ks[2], (DEPTH, D), f32),
        'lower_bounds': 0.1 * nrm(ks[3], (DEPTH, W), f32),
        'g_hgrn_out': 1.0 + 0.01 * nrm(ks[4], (DEPTH, W), f32),
        'w_conv': nrm(ks[5], (DEPTH, CONV_K, W), f32) * CONV_K ** -0.5,
        'sg_ln_g': 1.0 + 0.01 * nrm(ks[6], (DEPTH, W), f32),
        'sg_ln_b': 0.01 * nrm(ks[7], (DEPTH, W), f32),
        'w_sg': nrm(ks[8], (DEPTH, SG_GROUPS, SG_CHUNK, SG_CHUNK), f32) * SG_CHUNK ** -0.5,
        'b_sg': 1.0 + 0.01 * nrm(ks[9], (DEPTH, SG_GROUPS, SG_CHUNK), f32),
        'w_branch': nrm(ks[10], (DEPTH, N_BRANCH, W, D), f32) * W ** -0.5,
        'w_o': nrm(ks[11], (DEPTH, D, D), f32) * D ** -0.5,
        'g_ffn': 1.0 + 0.01 * nrm(ks[12], (DEPTH, D), f32),
        'w_ff1': nrm(ks[13], (DEPTH, D, D_FF), f32) * D ** -0.5,
        'w_ff2': nrm(ks[14], (DEPTH, D_FF, D), f32) * D_FF ** -0.5,
        'g_final': 1.0 + 0.01 * nrm(ks[15], (D,), f32),
    }


def reference(x, w_in, g_mix, lower_bounds, g_hgrn_out, w_conv, sg_ln_g, sg_ln_b,
              w_sg, b_sg, w_branch, w_o, g_ffn, w_ff1, w_ff2, g_final):
    B, S, D = x.shape
    W = BRANCH_WIDTH
    lbs = jax.nn.softmax(lower_bounds.astype(jnp.float32), axis=0)
    lbs = jnp.cumsum(lbs, axis=0) - lbs[0]
    offsets = [W * i for i in range(1, 10)]
    for l in range(DEPTH):
        h = rms_norm(x, g_mix[l])
        p = h @ w_in[l]
        q, fp, iv, go, bg, cg, xc, u, v, gates = jnp.split(p, offsets, axis=-1)
        o_a = hgrn2_mix(q, fp, iv, go, lbs[l], g_hgrn_out[l])
        o_b = short_conv_mix(bg, cg, xc, w_conv[l])
        o_c = spatial_gating_mix(u, v, sg_ln_g[l], sg_ln_b[l], w_sg[l], b_sg[l])
        z = jnp.stack([o_a, o_b, o_c], axis=2)
        y = jnp.einsum('bsnw,nwd->bsnd', z, w_branch[l])
        gate = jax.nn.sigmoid(gates).reshape(B, S, N_BRANCH, D)
        merged = jnp.sum(gate * y, axis=2)
        x = x + merged @ w_o[l]
        h2 = rms_norm(x, g_ffn[l])
        x = x + jnp.square(jax.nn.relu(h2 @ w_ff1[l])) @ w_ff2[l]
    return rms_norm(x, g_final)
```

```python
import numpy as np
from contextlib import ExitStack
import concourse.bass as bass
import concourse.mybir as mybir
from concourse.bass_utils import run_bass_kernel_spmd

F32 = mybir.dt.float32
BF16 = mybir.dt.bfloat16
AF = mybir.ActivationFunctionType
ALU = mybir.AluOpType
AX = mybir.AxisListType

D = 1024; S = 2048; W = 512; NL = 4; DFF = 4096
T = 1024
NHALF = S // T
TB = 512
NTB = T // TB
NS = 3
SLOT = 4608
NORM_EPS = 1e-6; LN_EPS = 1e-5
FG_POLICY = {"chain0": 2, "chain": 3, "pre": 1, "schain": 3, "omm": 1}
PREFETCH_N1 = True
EW2 = "dve"
FG_STEP = 3

def _block_sizes():
    bs = {"iv": (8, 512), "sgv": (8, 512), "u": (8, 512)}
    for i in range(4):
        bs["hg%d" % i] = (8, 384)
        bs["cv%d" % i] = (8, 384)
    for i in range(8):
        bs["GW%d" % i] = (1, 4608)
        bs["f1_%d" % i] = (8, 512)
        bs["f2_%d" % i] = (32, 128)
    for i in range(2):
        bs["o%d" % i] = (8, 512)
    return bs


BSZ = _block_sizes()
WTOT = sum(k * n for k, n in BSZ.values())
_ORDER = []


def _layout(order):
    off = {}
    o = 0
    for nm in order:
        kc, n = BSZ[nm]
        off[nm] = (o, kc, n)
        o += kc * n
    assert o == WTOT and len(order) == len(BSZ)
    return off


def _tile_cols(w, cols):
    K = w.shape[0]
    return w[:, cols].reshape(K // 128, 128, len(cols)).transpose(1, 0, 2)


def _pack_weights(w_in, w_branch, w_o, w_ff1, w_ff2, order):
    BOFF = _layout(order)
    out = np.empty((NL, 128, WTOT), np.float32)
    ar = np.arange
    for l in range(NL):
        def put(nm, arr):
            o, kc, n = BOFF[nm]
            out[l, :, o:o + kc * n] = arr.reshape(128, kc * n)
        wi = w_in[l]
        put("iv", _tile_cols(wi, ar(1024, 1536)))
        put("sgv", _tile_cols(wi, ar(4096, 4608)))
        put("u", _tile_cols(wi, ar(3584, 4096)))
        for hd in range(4):
            cols = np.concatenate([ar(base + hd * 128, base + hd * 128 + 128) for base in (0, 512, 1536)])
            put("hg%d" % hd, _tile_cols(wi, cols))
        for c in range(4):
            cols = np.concatenate([ar(base + c * 128, base + c * 128 + 128) for base in (2560, 3072, 2048)])
            put("cv%d" % c, _tile_cols(wi, cols))
        for dc in range(8):
            gcols = np.concatenate([ar(4608 + n * 1024 + dc * 128, 4608 + n * 1024 + dc * 128 + 128) for n in range(3)])
            gpart = _tile_cols(wi, gcols).reshape(128, 8 * 384)
            wparts = [_tile_cols(w_branch[l, n], ar(dc * 128, dc * 128 + 128)) for n in range(3)]
            wpart = np.concatenate(wparts, axis=1).reshape(128, 12 * 128)
            put("GW%d" % dc, np.concatenate([gpart, wpart], axis=1))
        for i in range(2):
            put("o%d" % i, _tile_cols(w_o[l], ar(i * 512, i * 512 + 512)))
        for i in range(8):
            put("f1_%d" % i, _tile_cols(w_ff1[l], ar(i * 512, i * 512 + 512)))
        for i in range(8):
            put("f2_%d" % i, _tile_cols(w_ff2[l], ar(i * 128, i * 128 + 128)))
    return out


PC = {}
_c = 0
for _nm, _n in [("gmix", 32), ("gffn", 32), ("gfin", 8), ("lbraw", 16), ("gout", 16), ("wconv", 48),
                ("lng", 16), ("lnb", 16)]:
    PC[_nm] = _c
    _c += _n
NPRM = _c


def _pack_params(g_mix, lower_bounds, g_hgrn_out, w_conv, sg_ln_g, sg_ln_b, g_ffn, g_final):
    p = np.zeros((128, NPRM), np.float32)
    p[:, PC["gmix"]:PC["gmix"] + 32] = g_mix.reshape(NL, 8, 128).transpose(2, 0, 1).reshape(128, 32)
    p[:, PC["gffn"]:PC["gffn"] + 32] = g_ffn.reshape(NL, 8, 128).transpose(2, 0, 1).reshape(128, 32)
    p[:, PC["gfin"]:PC["gfin"] + 8] = g_final.reshape(8, 128).T
    p[:, PC["lbraw"]:PC["lbraw"] + 16] = lower_bounds.reshape(NL, 4, 128).transpose(2, 1, 0).reshape(128, 16)
    p[:, PC["gout"]:PC["gout"] + 16] = g_hgrn_out.reshape(NL, 4, 128).transpose(2, 0, 1).reshape(128, 16)
    p[:, PC["wconv"]:PC["wconv"] + 48] = w_conv.reshape(NL, 3, 4, 128).transpose(3, 0, 1, 2).reshape(128, 48)
    p[:, PC["lng"]:PC["lng"] + 16] = sg_ln_g.reshape(NL, 4, 128).transpose(2, 0, 1).reshape(128, 16)
    p[:, PC["lnb"]:PC["lnb"] + 16] = sg_ln_b.reshape(NL, 4, 128).transpose(2, 0, 1).reshape(128, 16)
    return p


def _consts():
    c = np.zeros((128, 128 * 3 + 512), np.float32)
    c[:, 0:128] = np.eye(128, dtype=np.float32)
    c[:, 128:256] = np.triu(np.ones((128, 128), np.float32))
    c[:, 256:384] = 1.0
    m = np.ones((128, 512), np.float32)
    m[:, ::64] = 0.0
    c[:, 384:896] = m
    return c


class Tk:
    __slots__ = ("name", "w", "r")

    def __init__(self, name):
        self.name = name
        self.w = None
        self.r = {}


class Prog:
    def __init__(self, nc, es):
        self.nc = nc
        self.es = es
        self.E = {"pe": nc.tensor, "act": nc.scalar, "dve": nc.vector, "pool": nc.gpsimd, "sp": nc.sync}
        self.q = {e: [] for e in self.E}
        self.sem = {}
        self.cnt = {}
        self.seen = {e: {} for e in self.E}
        for e in self.E:
            self.newsem(e)

    def newsem(self, name):
        self.sem[name] = self.es.enter_context(self.nc.semaphore("s_" + name))
        self.cnt[name] = 0

    def _waits(self, eng, reads, writes):
        need = {}

        def req(s, v):
            if s == eng and eng == "pe":
                return
            if v > need.get(s, 0):
                need[s] = v
        for t in reads:
            if t.w is not None:
                req(*t.w)
        for t in writes:
            if t.w is not None:
                req(*t.w)
            for s, v in t.r.items():
                req(s, v)
        for s, v in need.items():
            if self.seen[eng].get(s, 0) < v:
                self.seen[eng][s] = v
                self.q[eng].append(("wait", self.sem[s], v))

    def op(self, eng, fn, reads=(), writes=(), inc=True, **kw):
        self._waits(eng, reads, writes)
        nxt = self.cnt[eng] + 1
        self.q[eng].append(("op", fn, kw, self.sem[eng] if inc else None))
        if inc:
            self.cnt[eng] = nxt
        for t in writes:
            t.w = (eng, nxt)
            t.r = {}
        for t in reads:
            if t.r.get(eng, 0) < nxt:
                t.r[eng] = nxt

    def dma(self, qeng, semname, out_ap, in_ap, reads=(), writes=()):
        self._waits(qeng, reads, writes)
        self.cnt[semname] += 16
        v = self.cnt[semname]
        self.q[qeng].append(("dma", out_ap, in_ap, self.sem[semname]))
        for t in writes:
            t.w = (semname, v)
            t.r = {}
        for t in reads:
            t.r[semname] = v
        return v

    def wait_all(self, eng, semname):
        self.q[eng].append(("wait", self.sem[semname], self.cnt[semname]))

    def replay(self, block):
        def run(E, lst):
            for it in lst:
                if it[0] == "wait":
                    E.wait_ge(it[1], it[2])
                elif it[0] == "op":
                    ins = getattr(E, it[1])(**it[2])
                    if it[3] is not None:
                        ins.then_inc(it[3], 1)
                else:
                    E.dma_start(out=it[1], in_=it[2]).then_inc(it[3], 16)
        q = self.q

        @block.tensor
        def _(e):
            run(e, q["pe"])

        @block.scalar
        def _(e):
            run(e, q["act"])

        @block.vector
        def _(e):
            run(e, q["dve"])

        @block.gpsimd
        def _(e):
            run(e, q["pool"])

        @block.sync
        def _(e):
            run(e, q["sp"])


def build(n_layers=NL, final_norm=True, dbg=(), order=None):
    dry = order is None
    req_order = []
    BOFF = None if dry else _layout(order)
    nc = bass.Bass("TRN2", target_bir_lowering=False)
    xin = nc.dram_tensor("xT", [128, 8 * S], F32, kind="ExternalInput").ap()
    wst = nc.dram_tensor("wst", [n_layers, 128, WTOT], F32, kind="ExternalInput").ap()
    prm_d = nc.dram_tensor("prm", [128, NPRM], F32, kind="ExternalInput").ap()
    cst_d = nc.dram_tensor("cst", [128, 896], F32, kind="ExternalInput").ap()
    wsg_d = nc.dram_tensor("wsgT", [n_layers, 128, 512], F32, kind="ExternalInput").ap()
    bsg_d = nc.dram_tensor("bsgb", [n_layers, 128, 512], F32, kind="ExternalInput").ap()
    yout = nc.dram_tensor("yT", [128, 8 * S], F32, kind="ExternalOutput").ap()
    dbg_out = {}

    with ExitStack() as es:
        P = Prog(nc, es)
        for s in ["init", "xld0", "xld1", "st", "wsg", "st0", "st1", "st2"] + ["ws%d" % i for i in range(NS)]:
            P.newsem(s)

        def sb(name, shape, dt):
            return es.enter_context(nc.sbuf_tensor(name, shape, dt))

        xT = sb("xT_sb", [128, 8, S], F32)
        hT = sb("hT", [128, 8, T], BF16)
        wsl = [sb("wsl%d" % i, [128, SLOT], BF16) for i in range(NS)]
        prm = sb("prm_sb", [128, NPRM], F32)
        cst = sb("cst_sb", [128, 768], F32)
        ident = sb("ident", [128, 128], BF16)
        cmask = cst[:, 0:128]
        smask = cst[:, 256:768]
        onesD = sb("onesD", [128, 128], BF16)
        ones128 = sb("ones128", [128, 128], BF16)
        ones1 = sb("ones1", [128, 128], BF16)
        epsT = sb("epsT", [128, 2], F32)
        lbT = sb("lbT", [128, 16, 4], F32)
        lbtmp = sb("lbtmp", [128, 16], F32)
        lbsum = sb("lbsum", [128, 4], F32)
        Sst = sb("Sst", [128, 4, 128], F32)
        halo = sb("halo", [128, 4, 2], F32)
        WmT = sb("WmT", [128, 512], BF16)
        Rg = sb("Rg", [128, 512], F32)
        sqb = [sb("sqb%d" % i, [128, TB], BF16) for i in range(2)]
        KB = 1024
        BIGB = 86 * KB
        big = sb("big", [128, BIGB // 4], F32)
        bigT = {}

        def carve(off, shape, dt):
            esz = 4 if dt == F32 else 2
            n = int(np.prod(shape))
            assert off % 4 == 0 and off + n * esz <= BIGB, (off, shape)
            base = big[:, off // 4:(off + n * esz) // 4]
            if dt != F32:
                base = base.bitcast(dt)
            if len(shape) == 2:
                base = base.rearrange("p (a b) -> p a b", a=shape[0])
            return base, off + n * esz

        zT, o = carve(0, [12, T], BF16)
        vtok, o = carve(24 * KB, [16, 512], BF16)
        gvb, o = carve(40 * KB, [8, 512], BF16)
        gu, o = carve(o, [4, T], BF16)
        cgs, o2 = carve(40 * KB, [1, T], F32)
        zc, o2 = carve(o2, [1, T + 4], F32)
        ycv, o2 = carve(o2, [1, T], F32)
        assert o2 <= 56 * KB
        ut8, _ = carve(56 * KB, [8, 128], F32)
        shist, _ = carve(60 * KB, [8, 128], F32)
        lfb, o = carve(56 * KB, [1, T], F32)
        bdb, o = carve(o, [1, T], F32)
        qsb, o = carve(o, [1, T], BF16)
        sgob, o = carve(o, [1, T], BF16)
        kTb, o = carve(o, [1, T], BF16)
        ktok, o = carve(o, [8, 128], BF16)
        sbf8, o = carve(o, [8, 128], BF16)
        scr, o = carve(o, [4, 512], F32)
        fgx, o = carve(o, [1, 512], F32)
        st1, o = carve(o, [1, 32], F32)
        atm8, o = carve(o, [8, 64], BF16)
        evec, o = carve(o, [4, 16], F32)
        assert o <= BIGB, o
        mergedT, o = carve(24 * KB, [8, T], BF16)
        gtb, o = carve(o, [3, 512], F32)
        macc, o = carve(o, [2, 512], F32)
        mtmp, o = carve(o, [2, 512], F32)
        assert o <= 56 * KB
        aT, o = carve(0, [32, T], BF16)
        rlb, o = carve(64 * KB, [3, 512], F32)
        sq4, o = carve(70 * KB, [4, 512], BF16)

        banks = [es.enter_context(nc.psum_tensor("bank%d" % i, [128, 512], F32)) for i in range(7)]
        ptb = es.enter_context(nc.psum_tensor("ptb", [128, 1024], BF16))
        bankT = [Tk("bank%d" % i) for i in range(7)]
        _b6 = Tk("bank6")
        b6T = [_b6, _b6, _b6, _b6]
        ptT = Tk("ptb")
        rot = [0]

        reserved = set()

        def nbank():
            i = rot[0]
            while i in reserved:
                i = (i + 1) % 7
            rot[0] = (i + 1) % 7
            return banks[i], bankT[i]

        tk = {}

        def K(name):
            if name not in tk:
                tk[name] = Tk(name)
            return tk[name]

        floor = {}

        def B(name):
            if name not in tk:
                t = Tk(name)
                t.r = dict(floor)
                tk[name] = t
            t = tk[name]
            bigT[name] = t
            return t

        def phase_barrier():
            agg = dict(floor)
            for t in bigT.values():
                if t.w is not None:
                    agg[t.w[0]] = max(agg.get(t.w[0], 0), t.w[1])
                for s_, v_ in t.r.items():
                    agg[s_] = max(agg.get(s_, 0), v_)
            floor.clear()
            floor.update(agg)
            for t in bigT.values():
                for s_, v_ in agg.items():
                    if t.r.get(s_, 0) < v_:
                        t.r[s_] = v_

        def inherit(new_names, old_names):
            agg = {}
            for nm in old_names:
                t = B(nm)
                if t.w is not None:
                    agg[t.w[0]] = max(agg.get(t.w[0], 0), t.w[1])
                for s_, v_ in t.r.items():
                    agg[s_] = max(agg.get(s_, 0), v_)
            for nm in new_names:
                t = B(nm)
                for s_, v_ in agg.items():
                    if t.r.get(s_, 0) < v_:
                        t.r[s_] = v_

        P.dma("sp", "init", prm[:], prm_d[:, :], writes=[K("prm")])
        v = P.dma("sp", "init", cst[:], cst_d[:, 128:896], writes=[K("cst")])
        K("prm").w = ("init", v)
        P.newsem("idl")
        P.dma("pool", "idl", ident[:], cst_d[:, 0:128], writes=[K("ident")])
        for h in range(NHALF):
            for kc in range(8):
                v = P.dma("sp", "xld%d" % h, xT[:, kc, h * T:(h + 1) * T], xin[:, kc * S + h * T: kc * S + (h + 1) * T],
                          writes=[K("x%d_%d" % (kc, h))])
            for kc in range(8):
                K("x%d_%d" % (kc, h)).w = ("xld%d" % h, v)

        P.op("dve", "memset", ap=onesD[:], constant=1.0 / 1024.0, writes=[K("onesD")])
        P.op("dve", "memset", ap=ones128[:], constant=1.0 / 128.0, writes=[K("ones128")])
        P.op("dve", "memset", ap=ones1[:], constant=1.0, writes=[K("ones1")])
        P.op("dve", "memset", ap=epsT[:, 0:1], constant=NORM_EPS, writes=[K("eps")])
        P.op("dve", "memset", ap=epsT[:, 1:2], constant=LN_EPS, writes=[K("eps")])
        P.op("dve", "memset", ap=Sst[:], constant=0.0, writes=[K("S%d" % i) for i in range(4)])
        P.op("dve", "memset", ap=halo[:], constant=0.0, writes=[K("halo")])
        P.op("dve", "memset", ap=lbT[:], constant=0.0, writes=[K("lbT")])
        lbr = prm[:, PC["lbraw"]:PC["lbraw"] + 16]
        P.op("act", "activation", out=lbtmp[:], in_=lbr, func=AF.Exp, reads=[K("prm")], writes=[K("lbtmp")])
        lb3 = lbtmp[:].rearrange("p (h l) -> p h l", l=4)
        P.op("dve", "tensor_reduce", out=lbsum[:], in_=lb3, axis=AX.X, op=ALU.add, reads=[K("lbtmp")], writes=[K("lbsum")])
        P.op("dve", "reciprocal", out=lbsum[:], in_=lbsum[:], reads=[K("lbsum")], writes=[K("lbsum")])
        P.op("dve", "tensor_tensor", out=lb3, in0=lb3, in1=lbsum[:].unsqueeze(2).to_broadcast([128, 4, 4]), op=ALU.mult,
             reads=[K("lbtmp"), K("lbsum")], writes=[K("lbtmp")])
        lbv = lbT[:].rearrange("p (h l) c -> p h l c", l=4)
        for l in range(1, 4):
            P.op("dve", "tensor_tensor", out=lbv[:, :, l, 0:1], in0=lbv[:, :, l - 1, 0:1], in1=lb3[:, :, l:l + 1], op=ALU.add,
                 reads=[K("lbT"), K("lbtmp")], writes=[K("lbT")])
        P.op("dve", "tensor_scalar", out=lbT[:, :, 1:2], in0=lbT[:, :, 0:1], scalar1=-1.0, scalar2=None, op0=ALU.add,
             reads=[K("lbT")], writes=[K("lbT")])
        P.op("act", "activation", out=lbT[:, :, 2:3], in_=lbT[:, :, 1:2], func=AF.Ln, scale=-1.0,
             reads=[K("lbT")], writes=[K("lbT")])

        stream = []
        if not dry:
            for l_ in range(n_layers):
                for h_ in range(NHALF):
                    for nm in order:
                        stream.append((l_, nm))
        wstate = {"issued": 0, "consumed": 0}
        slotT = [Tk("slot%d" % i) for i in range(NS)]
        slot_of = {}

        def issue_into(s_):
            j = wstate["issued"]
            if dry or j >= len(stream):
                return
            l_, nm = stream[j]
            o, kc, n = BOFF[nm]
            P.dma("pool", "ws%d" % s_, wsl[s_][:, 0:kc * n], wst[l_, :, o:o + kc * n], writes=[slotT[s_]])
            slot_of[j] = s_
            wstate["issued"] = j + 1

        for s_ in range(NS):
            issue_into(s_)

        class Blk:
            pass

        def next_block(nm):
            j = wstate["consumed"]
            wstate["consumed"] = j + 1
            kc, n = BSZ[nm]
            if dry:
                if len(req_order) < len(BSZ):
                    req_order.append(nm)
                s_ = j % NS
            else:
                assert j < wstate["issued"], ("block not prefetched", nm, j)
                assert stream[j][1] == nm, (stream[j], nm)
                s_ = slot_of.pop(j)
            b = Blk()
            b.slot = s_
            b.flat = wsl[s_]
            if kc > 1:
                b.view = wsl[s_][:, 0:kc * n].rearrange("p (k n) -> p k n", k=kc)
            b.tk = slotT[s_]
            return b

        def finish_block(blk):
            issue_into(blk.slot)

        def subblk(blk, view):
            b = Blk()
            b.view = view
            b.tk = blk.tk
            return b

        def colchunk(blk, c0, rhs_fn, rhs_tk, nk):
            bl = [nbank() for _ in range(NTB)]
            for kc in range(nk):
                for tb in range(NTB):
                    b, bt = bl[tb]
                    P.op("pe", "matmul", out=b[:, :], lhsT=blk.view[:, kc, c0:c0 + 128], rhs=rhs_fn(kc, tb),
                         start=(kc == 0), stop=(kc == nk - 1),
                         reads=[blk.tk] + rhs_tk(kc), writes=[bt], inc=(kc == nk - 1))
            return bl

        def colchunk_tb(blk, c0, rhs_fn, rhs_tk, nk, tb):
            b, bt = nbank()
            for kc in range(nk):
                P.op("pe", "matmul", out=b[:, :], lhsT=blk.view[:, kc, c0:c0 + 128], rhs=rhs_fn(kc, tb),
                     start=(kc == 0), stop=(kc == nk - 1),
                     reads=[blk.tk] + rhs_tk(kc), writes=[bt], inc=(kc == nk - 1))
            return b, bt

        def tsl(tb):
            return slice(tb * TB, (tb + 1) * TB)

        def h_rhs(kc, tb):
            return hT[:, kc, tsl(tb)]

        def h_tk(kc):
            return [K("hT%d" % kc)]

        def rmsnorm_a(h):
            xk = [K("x%d_%d" % (kc, h)) for kc in range(8)]
            bl = [nbank() for _ in range(NTB)]
            for kc in range(8):
                for tb in range(NTB):
                    i = (kc * NTB + tb) % 2
                    sl = slice(h * T + tb * TB, h * T + (tb + 1) * TB)
                    P.op("act", "activation", out=sqb[i][:], in_=xT[:, kc, sl], func=AF.Square,
                         reads=[xk[kc]], writes=[K("sqb%d" % i)])
                    b, bt = bl[tb]
                    P.op("pe", "matmul", out=b[:, :], lhsT=onesD[:], rhs=sqb[i][:], start=(kc == 0), stop=(kc == 7),
                         reads=[K("onesD"), K("sqb%d" % i)], writes=[bt])
            for tb in range(NTB):
                b, bt = bl[tb]
                rs = scr[:, 1 + tb, :]
                P.op("act", "activation", out=rs, in_=b[:, :], func=AF.Ln, bias=epsT[:, 0:1],
                     reads=[bt, K("eps")], writes=[B("scr%d" % (1 + tb))])
                P.op("act", "activation", out=rs, in_=rs, func=AF.Exp, scale=-0.5,
                     reads=[B("scr%d" % (1 + tb))], writes=[B("scr%d" % (1 + tb))])

        def rmsnorm_b(h, emit_out):
            xk = [K("x%d_%d" % (kc, h)) for kc in range(8)]
            for tb in range(NTB):
                for kc in range(8):
                    emit_out(kc, tb, xk[kc])

        def rmsnorm(h, emit_out):
            rmsnorm_a(h)
            rmsnorm_b(h, emit_out)

        def dump(name, ap, tks, shape, dt):
            if name in dbg and name not in dbg_out:
                d = nc.dram_tensor("dbg_" + name, list(shape), dt, kind="ExternalOutput").ap()
                dbg_out[name] = d
                P.dma("sp", "st", d, ap, reads=tks)

        oi = [0]
        fin_pre = {}

        def out_f(kc, tb, xk, h):
            r_ = oi[0] % 3
            oi[0] += 1
            sl = slice(h * T + tb * TB, h * T + (tb + 1) * TB)
            gc = PC["gfin"] + kc
            P.op("dve", "scalar_tensor_tensor", out=rlb[:, r_, :], in0=xT[:, kc, sl], scalar=prm[:, gc:gc + 1],
                 in1=scr[:, 1 + tb, :], op0=ALU.mult, op1=ALU.mult,
                 reads=[xk, K("prm"), B("scr%d" % (1 + tb))], writes=[B("rl%d" % r_)])
            P.dma("sp", "st%d" % r_, yout[:, kc * S + h * T + tb * TB: kc * S + h * T + (tb + 1) * TB], rlb[:, r_, :],
                  reads=[B("rl%d" % r_)])

        n1_pre = {}
        for l in range(n_layers):
            P.dma("sp", "wsg", scr[:, 1, :], wsg_d[l, :, :], writes=[B("scr1")])
            v = P.dma("sp", "wsg", scr[:, 2, :], bsg_d[l, :, :], writes=[B("scr2")])
            B("scr1").w = ("wsg", v)
            P.op("dve", "tensor_tensor", out=WmT[:].rearrange("p (g t) -> p g t", g=4),
                 in0=scr[:, 1, :].rearrange("p (g t) -> p g t", g=4),
                 in1=cmask.unsqueeze(1).to_broadcast([128, 4, 128]), op=ALU.mult,
                 reads=[B("scr1"), K("cst")], writes=[K("WmT")])
            rb, rbt = nbank()
            P.op("pe", "matmul", out=rb[:, :], lhsT=ones1[:], rhs=WmT[:], start=True, stop=True,
                 reads=[K("ones1"), K("WmT")], writes=[rbt])
            for g in range(4):
                gs = slice(g * 128, (g + 1) * 128)
                cb = PC["lnb"] + l * 4 + g
                P.op("dve", "scalar_tensor_tensor", out=Rg[:, gs], in0=rb[:, gs], scalar=prm[:, cb:cb + 1],
                     in1=scr[:, 2, gs], op0=ALU.mult, op1=ALU.add,
                     reads=[rbt, K("prm"), B("scr2")], writes=[K("Rg")])

            for h in range(NHALF):
                t0 = h * T
                phase_barrier()
                if h == 0 and l > 0:
                    P.op("dve", "memset", ap=Sst[:], constant=0.0, writes=[K("S%d" % i) for i in range(4)])
                    P.op("dve", "memset", ap=halo[:], constant=0.0, writes=[K("halo")])
                gmc = PC["gmix"] + l * 8

                def out_h(kc, tb, xk, gbase=gmc, h=h):
                    sl = slice(h * T + tb * TB, h * T + (tb + 1) * TB)
                    P.op("dve", "scalar_tensor_tensor", out=hT[:, kc, tsl(tb)], in0=xT[:, kc, sl],
                         scalar=prm[:, gbase + kc:gbase + kc + 1], in1=scr[:, 1 + tb, :], op0=ALU.mult, op1=ALU.mult,
                         reads=[xk, K("prm"), B("scr%d" % (1 + tb))], writes=[K("hT%d" % kc)])
                if not n1_pre.pop((l, h), False):
                    rmsnorm(h, out_h)
                if l == 0 and h == 0:
                    dump("hT", hT[:], [K("hT%d" % kc) for kc in range(8)], [128, 8, T], BF16)

                fg = []
                fst = {}

                def iv_unit(g4):
                    if g4 == 0:
                        fst["iv"] = next_block("iv")
                    blk = fst["iv"]
                    for c in range(g4 * 4, g4 * 4 + 4):
                        b, bt = nbank()
                        for kc in range(8):
                            P.op("pe", "matmul", out=b[0:64, :], lhsT=hT[:, kc, c * 64:(c + 1) * 64], rhs=blk.view[:, kc, :],
                                 start=(kc == 0), stop=(kc == 7),
                                 reads=[blk.tk, K("hT%d" % kc)], writes=[bt], inc=(kc == 7))
                        P.op("act", "activation", out=vtok[0:64, c, :], in_=b[0:64, :], func=AF.Copy, reads=[bt], writes=[B("vtok%d" % c)])
                    if g4 == 3:
                        finish_block(blk)

                for g4 in range(4):
                    fg.append(lambda g4=g4: iv_unit(g4))

                def sgv_unit(tt, l=l):
                    if tt == 0:
                        fst["sgv"] = next_block("sgv")
                    blk = fst["sgv"]
                    b, bt = nbank()
                    for kc in range(8):
                        P.op("pe", "matmul", out=b[:, :], lhsT=hT[:, kc, tt * 128:(tt + 1) * 128], rhs=blk.view[:, kc, :],
                             start=(kc == 0), stop=(kc == 7),
                             reads=[blk.tk, K("hT%d" % kc)], writes=[bt], inc=(kc == 7))
                    P.op("act", "activation", out=scr[:, 0, :], in_=b[:, :], func=AF.Gelu_apprx_tanh,
                         reads=[bt], writes=[B("scr0")])
                    P.op("dve", "tensor_reduce", out=st1[:, 0, tt:tt + 1], in_=scr[:, 0, :], axis=AX.X, op=ALU.add,
                         reads=[B("scr0")], writes=[B("st1")])
                    P.op(EW2, "tensor_tensor", out=fgx[:, 0, :], in0=scr[:, 0, :], in1=scr[:, 0, :], op=ALU.mult,
                         reads=[B("scr0")], writes=[B("fgx")])
                    P.op("dve", "tensor_reduce", out=st1[:, 0, 8 + tt:9 + tt], in_=fgx[:, 0, :], axis=AX.X, op=ALU.add,
                         reads=[B("fgx")], writes=[B("st1")])
                    P.op("act", "activation", out=gvb[:, tt, :], in_=scr[:, 0, :], func=AF.Copy, reads=[B("scr0")], writes=[B("gv%d" % tt)])
                    if tt == 7:
                        finish_block(blk)
                        P.op("dve", "tensor_scalar", out=st1[:, 0, 16:24], in0=st1[:, 0, 0:8], scalar1=1.0 / 512.0, scalar2=None,
                             op0=ALU.mult, reads=[B("st1")], writes=[B("st1")])
                        P.op("dve", "tensor_tensor", out=st1[:, 0, 0:8], in0=st1[:, 0, 16:24], in1=st1[:, 0, 16:24], op=ALU.mult,
                             reads=[B("st1")], writes=[B("st1")])
                        P.op("dve", "scalar_tensor_tensor", out=st1[:, 0, 24:32], in0=st1[:, 0, 8:16], scalar=1.0 / 512.0,
                             in1=st1[:, 0, 0:8], op0=ALU.mult, op1=ALU.subtract, reads=[B("st1")], writes=[B("st1")])
                        P.op("act", "activation", out=st1[:, 0, 24:32], in_=st1[:, 0, 24:32], func=AF.Ln, bias=epsT[:, 1:2],
                             reads=[B("st1"), K("eps")], writes=[B("st1")])
                        P.op("act", "activation", out=st1[:, 0, 24:32], in_=st1[:, 0, 24:32], func=AF.Exp, scale=-0.5,
                             reads=[B("st1")], writes=[B("st1")])
                        P.op("dve", "scalar_tensor_tensor", out=st1[:, 0, 0:8], in0=st1[:, 0, 16:24], scalar=-1.0,
                             in1=st1[:, 0, 24:32], op0=ALU.mult, op1=ALU.mult, reads=[B("st1")], writes=[B("st1")])
                        for t2 in range(8):
                            P.op("act", "activation", out=gvb[:, t2, :], in_=gvb[:, t2, :], func=AF.Identity,
                                 scale=st1[:, 0, 24 + t2:25 + t2], bias=st1[:, 0, t2:t2 + 1],
                                 reads=[B("gv%d" % t2), B("st1")], writes=[B("gv%d" % t2)])

                def u_unit(g, tb):
                    if g == 0 and tb == 0:
                        fst["u"] = next_block("u")
                    blk = fst["u"]
                    b, bt = colchunk_tb(blk, g * 128, h_rhs, h_tk, 8, tb)
                    P.op("act", "activation", out=gu[:, g, tsl(tb)], in_=b[:, :], func=AF.Gelu_apprx_tanh,
                         reads=[bt], writes=[B("gu%d_%d" % (g, tb))])
                    if g == 3 and tb == NTB - 1:
                        finish_block(blk)

                def m1t_unit(g, tb, l=l):
                    gs = slice(g * 128, (g + 1) * 128)
                    cg_ = PC["lng"] + l * 4 + g
                    b, bt = nbank()
                    for j in range(4):
                        tt = tb * 4 + j
                        P.op("pe", "matmul", out=b[:, j * 128:(j + 1) * 128], lhsT=gvb[:, tt, gs], rhs=WmT[:, gs],
                             start=True, stop=True, reads=[B("gv%d" % tt), K("WmT")], writes=[bt], inc=(j == 3))
                    sv_ap = scr[:, 0, :] if tb == 0 else fgx[:, 0, :]
                    SV = B("scr0") if tb == 0 else B("fgx")
                    P.op("dve", "scalar_tensor_tensor", out=sv_ap.rearrange("p (j t) -> p j t", j=4),
                         in0=b[:, :].rearrange("p (j t) -> p j t", j=4), scalar=prm[:, cg_:cg_ + 1],
                         in1=Rg[:, gs].unsqueeze(1).to_broadcast([128, 4, 128]), op0=ALU.mult, op1=ALU.add,
                         reads=[bt, K("prm"), K("Rg")], writes=[SV])
                    P.op(EW2, "tensor_tensor", out=zT[:, 8 + g, tsl(tb)], in0=sv_ap, in1=gu[:, g, tsl(tb)],
                         op=ALU.mult, reads=[SV, B("gu%d_%d" % (g, tb))], writes=[B("zT%d" % (8 + g))])

                def conv_unit(cc, which, tb, l=l):
                    def wc(k):
                        c_ = PC["wconv"] + l * 12 + k * 4 + cc
                        return prm[:, c_:c_ + 1]
                    if which == 0 and tb == 0:
                        fst["cv"] = next_block("cv%d" % cc)
                        if cc == 0:
                            inherit(["cgs0", "cgs1", "zc", "ycv"],
                                    ["gv%d" % i for i in range(8)] + ["gu%d_%d" % (g_, t_) for g_ in range(4) for t_ in range(NTB)])
                    blk = fst["cv"]
                    b, bt = colchunk_tb(blk, which * 128, h_rhs, h_tk, 8, tb)
                    if which == 0:
                        P.op("act", "activation", out=cgs[:, 0, tsl(tb)], in_=b[:, :], func=AF.Copy, reads=[bt], writes=[B("cgs%d" % tb)])
                    elif which == 1:
                        if tb == 0:
                            P.op("dve", "tensor_copy", out=zc[:, 0, 0:2], in_=halo[:, cc, :], reads=[K("halo")], writes=[B("zc")])
                        P.op("dve", "tensor_tensor", out=zc[:, 0, 2 + tb * TB:2 + (tb + 1) * TB], in0=b[:, :], in1=cgs[:, 0, tsl(tb)],
                             op=ALU.mult, reads=[bt, B("cgs%d" % tb)], writes=[B("zc")])
                        if tb == NTB - 1:
                            P.op("dve", "tensor_copy", out=halo[:, cc, :], in_=zc[:, 0, T:T + 2], reads=[B("zc")], writes=[K("halo")])
                            P.op("act", "activation", out=ycv[:, 0, :], in_=zc[:, 0, 2:T + 2], func=AF.Copy, scale=wc(2),
                                 reads=[B("zc"), K("prm")], writes=[B("ycv")])
                            P.op("dve", "scalar_tensor_tensor", out=ycv[:, 0, :], in0=zc[:, 0, 1:T + 1], scalar=wc(1), in1=ycv[:, 0, :],
                                 op0=ALU.mult, op1=ALU.add, reads=[B("zc"), K("prm"), B("ycv")], writes=[B("ycv")])
                            P.op("dve", "scalar_tensor_tensor", out=ycv[:, 0, :], in0=zc[:, 0, 0:T], scalar=wc(0), in1=ycv[:, 0, :],
                                 op0=ALU.mult, op1=ALU.add, reads=[B("zc"), K("prm"), B("ycv")], writes=[B("ycv")])
                    else:
                        if tb == NTB - 1:
                            finish_block(blk)
                        P.op("dve", "tensor_tensor", out=zT[:, 4 + cc, tsl(tb)], in0=b[:, :], in1=ycv[:, 0, tsl(tb)], op=ALU.mult,
                             reads=[bt, B("ycv")], writes=[B("zT%d" % (4 + cc))])

                for tt in range(8):
                    fg.append(lambda tt=tt: sgv_unit(tt))
                for g in range(4):
                    for tb in range(NTB):
                        fg.append(lambda g=g, tb=tb: u_unit(g, tb))
                for g in range(4):
                    for tb in range(NTB):
                        fg.append(lambda g=g, tb=tb: m1t_unit(g, tb))
                for cc in range(4):
                    for which in range(3):
                        for tb in range(NTB):
                            fg.append(lambda cc=cc, which=which, tb=tb: conv_unit(cc, which, tb))

                def hgrn_gen(l=l, h=h):
                    for hd in range(4):
                        lbi = hd * 4 + l
                        nom = lbT[:, lbi, 1:2]
                        lnom = lbT[:, lbi, 2:3]
                        sT = K("S%d" % hd)
                        hs = slice(hd * 128, (hd + 1) * 128)
                        inherit(["lf0", "lf1"], ["ut8"])
                        inherit(["bd0", "bd1"], ["shist"])
                        blk = next_block("hg%d" % hd)
                        bl = colchunk(blk, 0, h_rhs, h_tk, 8)
                        for tb in range(NTB):
                            b, bt = bl[tb]
                            P.op("act", "activation", out=scr[:, 1, :], in_=b[:, :], func=AF.Sigmoid, reads=[bt], writes=[B("scr1")])
                            P.op("dve", "tensor_tensor", out=qsb[:, 0, tsl(tb)], in0=b[:, :], in1=scr[:, 1, :], op=ALU.mult,
                                 reads=[bt, B("scr1")], writes=[B("qs%d" % tb)])
                        yield "proj"
                        bl = colchunk(blk, 128, h_rhs, h_tk, 8)
                        for tb in range(NTB):
                            b, bt = bl[tb]
                            P.op("act", "activation", out=lfb[:, 0, tsl(tb)], in_=b[:, :], func=AF.Sigmoid, scale=-1.0,
                                 reads=[bt], writes=[B("lf%d" % tb)])
                        yield "proj"
                        bl = colchunk(blk, 256, h_rhs, h_tk, 8)
                        finish_block(blk)
                        for tb in range(NTB):
                            b, bt = bl[tb]
                            P.op("act", "activation", out=sgob[:, 0, tsl(tb)], in_=b[:, :], func=AF.Sigmoid,
                                 reads=[bt], writes=[B("sgo%d" % tb)])
                        yield "proj"
                        Xs = [lfb[:, 0, tsl(tb)] for tb in range(NTB)]
                        Ys = [bdb[:, 0, tsl(tb)] for tb in range(NTB)]
                        Zs = [scr[:, 2 + tb, :] for tb in range(NTB)]
                        XT = [B("lf%d" % tb) for tb in range(NTB)]
                        YT = [B("bd%d" % tb) for tb in range(NTB)]
                        ZT = [B("scr%d" % (2 + tb)) for tb in range(NTB)]
                        cks = [slice(tb * 8, (tb + 1) * 8) for tb in range(NTB)]
                        for tb in range(NTB):
                            P.op("act", "activation", out=Ys[tb], in_=Xs[tb], func=AF.Ln, scale=nom, bias=1.0,
                                 reads=[XT[tb], K("lbT")], writes=[YT[tb]])
                        z3 = [z.rearrange("p (c t) -> p c t", t=64) for z in Zs]
                        for tb in range(NTB):
                            P.op("dve", "tensor_tensor_scan", out=Zs[tb], data0=smask, data1=Ys[tb], initial=0.0, op0=ALU.mult, op1=ALU.add,
                                 reads=[YT[tb], K("cst")], writes=[ZT[tb]])
                        for tb in range(NTB):
                            P.op("dve", "tensor_tensor", out=evec[:, 0, cks[tb]].unsqueeze(2), in0=z3[tb][:, :, 63:64], in1=z3[tb][:, :, 31:32],
                                 op=ALU.subtract, reads=[ZT[tb]], writes=[B("ev0")])
                            P.op("dve", "tensor_tensor", out=Ys[tb].rearrange("p (c t) -> p c t", t=64), in0=z3[tb],
                                 in1=z3[tb][:, :, 31:32].to_broadcast([128, 8, 64]), op=ALU.subtract, reads=[ZT[tb], YT[tb]], writes=[YT[tb]])
                        for tb in range(NTB):
                            P.op("act", "activation", out=evec[:, 1, cks[tb]].unsqueeze(2), in_=z3[tb][:, :, 31:32], func=AF.Exp,
                                 reads=[ZT[tb]], writes=[B("ev1")])
                            P.op("act", "activation", out=evec[:, 2, cks[tb]].unsqueeze(2), in_=z3[tb][:, :, 63:64], func=AF.Exp,
                                 reads=[ZT[tb]], writes=[B("ev2")])
                        for tb in range(NTB):
                            P.op("act", "activation", out=scr[:, 1, :], in_=Ys[tb], func=AF.Exp, reads=[YT[tb]], writes=[B("scr1")])
                            P.op(EW2, "tensor_tensor", out=qsb[:, 0, tsl(tb)], in0=qsb[:, 0, tsl(tb)], in1=scr[:, 1, :], op=ALU.mult,
                                 reads=[B("qs%d" % tb), B("scr1")], writes=[B("qs%d" % tb)])
                        for tb in range(NTB):
                            P.op("act", "activation", out=Zs[tb], in_=Ys[tb], func=AF.Exp, scale=-1.0, bias=lnom,
                                 reads=[YT[tb], K("lbT")], writes=[ZT[tb]])
                            P.op(EW2, "tensor_tensor", out=kTb[:, 0, tsl(tb)], in0=Xs[tb], in1=Zs[tb], op=ALU.mult,
                                 reads=[XT[tb], ZT[tb]], writes=[B("kT%d" % tb)])
                        P.op("act", "activation", out=evec[:, 3, :], in_=evec[:, 0, :], func=AF.Exp,
                             reads=[B("ev0")], writes=[B("ev3")])
                        yield ("chain0" if hd == 0 else "chain")
                        inherit(["ut8"], ["lf0", "lf1"])
                        inherit(["shist"], ["bd0", "bd1"])
                        if l == 0 and h == 0 and hd == 0:
                            dump("qt", qsb[:, 0, :], [B("qs0"), B("qs1")], [128, T], BF16)
                            dump("kt", kTb[:, 0, :], [B("kT0"), B("kT1")], [128, T], BF16)
                        for tb in range(NTB):
                            ck = slice(tb * 8, (tb + 1) * 8)
                            for c8 in range(8):
                                c = tb * 8 + c8
                                P.op("pe", "transpose", out=ptb[0:64, c8 * 128:(c8 + 1) * 128], in_=kTb[:, 0, c * 64:(c + 1) * 64],
                                     identity=ident[:], reads=[B("kT%d" % tb), K("ident")], writes=[ptT], inc=(c8 == 7))
                            ab, abt = nbank()
                            for c8 in range(8):
                                cs = slice((tb * 8 + c8) * 64, (tb * 8 + c8 + 1) * 64)
                                P.op("pe", "matmul", out=ab[0:64, c8 * 64:(c8 + 1) * 64], lhsT=kTb[:, 0, cs], rhs=qsb[:, 0, cs],
                                     start=True, stop=True, reads=[B("kT%d" % tb), B("qs%d" % tb)], writes=[abt], inc=(c8 == 7))
                            P.op("dve", "tensor_tensor", out=atm8[0:64, :, :], in0=ab[0:64, :].rearrange("p (c t) -> p c t", c=8),
                                 in1=cmask[0:64, 0:64].unsqueeze(1).to_broadcast([64, 8, 64]), op=ALU.mult,
                                 reads=[abt, K("cst")], writes=[B("atm8")])
                            P.op("act", "activation", out=ktok[0:64, 0:8, :],
                                 in_=ptb[0:64, :].rearrange("p (c k) -> p c k", c=8), func=AF.Copy, reads=[ptT], writes=[B("ktok")])
                            yield "pre"
                            for half8 in range(2):
                                ub, ubt = nbank()
                                for j in range(4):
                                    c8 = half8 * 4 + j
                                    c = tb * 8 + c8
                                    P.op("pe", "matmul", out=ub[:, j * 128:(j + 1) * 128], lhsT=ktok[0:64, c8, :],
                                         rhs=vtok[0:64, c, hs], start=True, stop=True,
                                         reads=[B("ktok"), B("vtok%d" % c)], writes=[ubt], inc=(j == 3))
                                c0 = tb * 8 + half8 * 4
                                P.op("dve", "tensor_tensor", out=ut8[:, half8 * 4:half8 * 4 + 4, :],
                                     in0=ub[:, :].rearrange("p (c v) -> p c v", c=4),
                                     in1=evec[:, 3, c0:c0 + 4].unsqueeze(2).to_broadcast([128, 4, 128]), op=ALU.mult,
                                     reads=[ubt, B("ev3")], writes=[B("ut8")])
                            P.op("dve", "tensor_copy", out=shist[:, 0, :], in_=Sst[:, hd, :], reads=[sT], writes=[B("shist")])
                            for c8 in range(8):
                                c = tb * 8 + c8
                                if c8 < 7:
                                    P.op("dve", "scalar_tensor_tensor", out=shist[:, c8 + 1, :], in0=shist[:, c8, :],
                                         scalar=evec[:, 2, c:c + 1], in1=ut8[:, c8, :], op0=ALU.mult, op1=ALU.add,
                                         reads=[B("shist"), B("ut8"), B("ev2")], writes=[B("shist")])
                                else:
                                    P.op("dve", "scalar_tensor_tensor", out=Sst[:, hd, :], in0=shist[:, c8, :],
                                         scalar=evec[:, 2, c:c + 1], in1=ut8[:, c8, :], op0=ALU.mult, op1=ALU.add,
                                         reads=[B("shist"), B("ut8"), B("ev2")], writes=[sT])
                            P.op("dve", "tensor_tensor", out=sbf8[:, :, :], in0=shist[:, :, :],
                                 in1=evec[:, 1, ck].unsqueeze(2).to_broadcast([128, 8, 128]), op=ALU.mult,
                                 reads=[B("shist"), B("ev1")], writes=[B("sbf8")])
                            yield "schain"
                            ob, obt = nbank()
                            for c8 in range(8):
                                c = tb * 8 + c8
                                cs = slice(c * 64, (c + 1) * 64)
                                P.op("pe", "matmul", out=ob[:, c8 * 64:(c8 + 1) * 64], lhsT=vtok[0:64, c, hs], rhs=atm8[0:64, c8, :],
                                     start=True, stop=False, reads=[B("vtok%d" % c), B("atm8")], writes=[obt], inc=False)
                                P.op("pe", "matmul", out=ob[:, c8 * 64:(c8 + 1) * 64], lhsT=sbf8[:, c8, :], rhs=qsb[:, 0, cs],
                                     start=False, stop=True, reads=[B("sbf8"), B("qs%d" % tb)], writes=[obt], inc=(c8 == 7))
                            P.op("act", "activation", out=sqb[0][:], in_=ob[:, :], func=AF.Square, reads=[obt], writes=[K("sqb0")])
                            yield "omm"
                            mb, mbt = nbank()
                            P.op("pe", "matmul", out=mb[:, :], lhsT=ones128[:], rhs=sqb[0][:], start=True, stop=True,
                                 reads=[K("ones128"), K("sqb0")], writes=[mbt])
                            P.op("act", "activation", out=scr[:, 1, :], in_=mb[:, :], func=AF.Ln, bias=epsT[:, 0:1],
                                 reads=[mbt, K("eps")], writes=[B("scr1")])
                            P.op("act", "activation", out=scr[:, 1, :], in_=scr[:, 1, :], func=AF.Exp, scale=-0.5,
                                 reads=[B("scr1")], writes=[B("scr1")])
                            cgo = PC["gout"] + l * 4 + hd
                            P.op("dve", "scalar_tensor_tensor", out=scr[:, 3, :], in0=ob[:, :], scalar=prm[:, cgo:cgo + 1],
                                 in1=scr[:, 1, :], op0=ALU.mult, op1=ALU.mult, reads=[obt, K("prm"), B("scr1")], writes=[B("scr3")])
                            P.op("dve", "tensor_tensor", out=zT[:, hd, tsl(tb)], in0=scr[:, 3, :], in1=sgob[:, 0, tsl(tb)], op=ALU.mult,
                                 reads=[B("scr3"), B("sgo%d" % tb)], writes=[B("zT%d" % hd)])

                fgi = 0
                stepn = 0
                for kind in hgrn_gen():
                    nfg = 0
                    nfg = FG_POLICY.get(kind, 0)
                    for _ in range(nfg):
                        if fgi < len(fg):
                            fg[fgi]()
                            fgi += 1
                while fgi < len(fg):
                    fg[fgi]()
                    fgi += 1
                if l == 0 and h == 0:
                    dump("zT", zT[:], [B("zT%d" % i) for i in range(12)], [128, 12, T], BF16)

                phase_barrier()
                gtc = 0
                for dc in range(8):
                    blk = next_block("GW%d" % dc)
                    gview = subblk(blk, blk.flat[:, 0:3072].rearrange("p (k n) -> p k n", k=8))
                    wview = blk.flat[:, 3072:4608].rearrange("p (k n) -> p k n", k=12)
                    for n in range(3):
                        gl = colchunk(gview, n * 128, h_rhs, h_tk, 8)

                        def z_rhs(kc, tb, n=n):
                            return zT[:, n * 4 + kc, tsl(tb)]

                        def z_tk(kc, n=n):
                            return [B("zT%d" % (n * 4 + kc))]
                        yl = colchunk(subblk(blk, wview[:, n * 4:(n + 1) * 4, :]), 0, z_rhs, z_tk, 4)
                        for tb in range(NTB):
                            g_b, g_bt = gl[tb]
                            y_b, y_bt = yl[tb]
                            gi_ = gtc % 3
                            gtc += 1
                            GT = B("gt%d" % gi_)
                            P.op("act", "activation", out=gtb[:, gi_, :], in_=g_b[:, :], func=AF.Sigmoid, reads=[g_bt], writes=[GT])
                            if n == 0:
                                P.op("dve", "tensor_tensor", out=macc[:, tb, :], in0=y_b[:, :], in1=gtb[:, gi_, :], op=ALU.mult,
                                     reads=[y_bt, GT], writes=[B("macc%d" % tb)])
                            else:
                                P.op("dve", "tensor_tensor", out=mtmp[:, tb, :], in0=y_b[:, :], in1=gtb[:, gi_, :], op=ALU.mult,
                                     reads=[y_bt, GT], writes=[B("mtmp%d" % tb)])
                                if n == 1:
                                    P.op("dve", "tensor_tensor", out=macc[:, tb, :], in0=macc[:, tb, :], in1=mtmp[:, tb, :], op=ALU.add,
                                         reads=[B("macc%d" % tb), B("mtmp%d" % tb)], writes=[B("macc%d" % tb)])
                                else:
                                    P.op("dve", "tensor_tensor", out=mergedT[:, dc, tsl(tb)], in0=macc[:, tb, :], in1=mtmp[:, tb, :],
                                         op=ALU.add, reads=[B("macc%d" % tb), B("mtmp%d" % tb)], writes=[B("mg%d" % dc)])
                    finish_block(blk)

                def m_rhs(kc, tb):
                    return mergedT[:, kc, tsl(tb)]

                def m_tk(kc):
                    return [B("mg%d" % kc)]
                gfc = PC["gffn"] + l * 8
                nb_ = [nbank() for _ in range(NTB)]
                for b_, bt_ in nb_:
                    reserved.add(banks.index(b_))
                sqi = 0
                pend = []

                def ms_mm(oc, tb, i):
                    P.op("pe", "matmul", out=nb_[tb][0][:, :], lhsT=onesD[:], rhs=sq4[:, i, :], start=(oc == 0), stop=(oc == 7),
                         reads=[K("onesD"), B("sq4_%d" % i)], writes=[nb_[tb][1]])
                for ob_ in range(2):
                    blk = next_block("o%d" % ob_)
                    for j in range(4):
                        oc = ob_ * 4 + j
                        bl = colchunk(blk, j * 128, m_rhs, m_tk, 8)
                        XK = K("x%d_%d" % (oc, h))
                        for tb in range(NTB):
                            b, bt = bl[tb]
                            sl = slice(t0 + tb * TB, t0 + (tb + 1) * TB)
                            P.op("dve", "tensor_tensor", out=xT[:, oc, sl], in0=b[:, :], in1=xT[:, oc, sl], op=ALU.add,
                                 reads=[bt, XK], writes=[XK])
                        for tb in range(NTB):
                            sl = slice(t0 + tb * TB, t0 + (tb + 1) * TB)
                            P.op("dve", "tensor_scalar", out=hT[:, oc, tsl(tb)], in0=xT[:, oc, sl], scalar1=prm[:, gfc + oc:gfc + oc + 1],
                                 scalar2=None, op0=ALU.mult, reads=[XK, K("prm")], writes=[K("hT%d" % oc)])
                            i = sqi % 4
                            sqi += 1
                            P.op("act", "activation", out=sq4[:, i, :], in_=xT[:, oc, sl], func=AF.Square,
                                 reads=[XK], writes=[B("sq4_%d" % i)])
                            pend.append((oc, tb, i))
                        while len(pend) > NTB:
                            ms_mm(*pend.pop(0))
                    finish_block(blk)

                def finish_r2():
                    while pend:
                        ms_mm(*pend.pop(0))
                    r2_tail()

                def r2_tail():
                  for tb in range(NTB):
                      b_, bt_ = nb_[tb]
                      rs = scr[:, 1 + tb, :]
                      P.op("act", "activation", out=rs, in_=b_[:, :], func=AF.Ln, bias=epsT[:, 0:1],
                           reads=[bt_, K("eps")], writes=[B("scr%d" % (1 + tb))])
                      P.op("act", "activation", out=rs, in_=rs, func=AF.Exp, scale=-1.0,
                           reads=[B("scr%d" % (1 + tb))], writes=[B("scr%d" % (1 + tb))])
                      reserved.discard(banks.index(b_))
                r2_state = {"done": False}
                if l == 0 and h == 0:
                    dump("x1", xT[:, :, 0:T], [K("x%d_0" % i) for i in range(8)], [128, 8, T], F32)
                phase_barrier()
                phase_barrier()
                ri = 0
                for fb in range(8):
                    blk = next_block("f1_%d" % fb)
                    for j in range(4):
                        jc = fb * 4 + j
                        bl = colchunk(blk, j * 128, h_rhs, h_tk, 8)
                        if not r2_state["done"]:
                            r2_state["done"] = True
                            finish_r2()
                        for tb in range(NTB):
                            b, bt = bl[tb]
                            r_ = ri % 3
                            ri += 1
                            P.op("act", "activation", out=rlb[:, r_, :], in_=b[:, :], func=AF.Relu, reads=[bt], writes=[B("rl%d" % r_)])
                            P.op("act", "activation", out=rlb[:, r_, :], in_=rlb[:, r_, :], func=AF.Square,
                                 reads=[B("rl%d" % r_)], writes=[B("rl%d" % r_)])
                            P.op("dve", "tensor_tensor", out=aT[:, jc, tsl(tb)], in0=rlb[:, r_, :], in1=scr[:, 1 + tb, :], op=ALU.mult,
                                 reads=[B("rl%d" % r_), B("scr%d" % (1 + tb))], writes=[B("aT%d" % jc)])
                    finish_block(blk)

                def a_rhs(kc, tb):
                    return aT[:, kc, tsl(tb)]

                def a_tk(kc):
                    return [B("aT%d" % kc)]
                for oc in range(8):
                    blk = next_block("f2_%d" % oc)
                    bl = colchunk(blk, 0, a_rhs, a_tk, 32)
                    for tb in range(NTB):
                        b, bt = bl[tb]
                        sl = slice(t0 + tb * TB, t0 + (tb + 1) * TB)
                        XK = K("x%d_%d" % (oc, h))
                        P.op("dve", "tensor_tensor", out=xT[:, oc, sl], in0=b[:, :], in1=xT[:, oc, sl], op=ALU.add,
                             reads=[bt, XK], writes=[XK])
                    finish_block(blk)
                    nxt = (l, h + 1) if h + 1 < NHALF else ((l + 1, 0) if l + 1 < n_layers else None)
                    if PREFETCH_N1 and nxt is None and final_norm and NHALF == 2 and h == 1:
                        if oc == 1:
                            rmsnorm_a(0)
                        elif oc == 2:
                            rmsnorm_b(0, lambda kc, tb, xk: out_f(kc, tb, xk, 0))
                            fin_pre[0] = True
                    if PREFETCH_N1 and nxt is not None:
                        if oc == 1:
                            rmsnorm_a(nxt[1])
                        elif oc == 2:
                            gb_ = PC["gmix"] + nxt[0] * 8
                            rmsnorm_b(nxt[1], lambda kc, tb, xk, gb_=gb_, nh=nxt[1]: out_h(kc, tb, xk, gb_, nh))
                            n1_pre[nxt] = True

        for h in range(NHALF):
            if final_norm:
                if not fin_pre.get(h):
                    rmsnorm(h, lambda kc, tb, xk, h=h: out_f(kc, tb, xk, h))
            else:
                for kc in range(8):
                    P.dma("sp", "st", yout[:, kc * S + h * T: kc * S + (h + 1) * T], xT[:, kc, h * T:(h + 1) * T],
                          reads=[K("x%d_%d" % (kc, h))])
        for s_ in ["st", "st0", "st1", "st2"]:
            if P.cnt[s_] > 0:
                P.wait_all("sp", s_)
        if dry:
            return None, list(req_order)
        block = es.enter_context(nc.Block())
        P.replay(block)
    return nc, dbg_out


def get_order():
    if not _ORDER:
        _ORDER.extend(build(n_layers=1, final_norm=False, order=None)[1])
    return list(_ORDER)


_CACHE = {}


def _prep_shared(inputs):
    wst = _pack_weights(inputs["w_in"], inputs["w_branch"], inputs["w_o"], inputs["w_ff1"], inputs["w_ff2"], get_order())
    prm = _pack_params(inputs["g_mix"], inputs["lower_bounds"], inputs["g_hgrn_out"], inputs["w_conv"],
                       inputs["sg_ln_g"], inputs["sg_ln_b"], inputs["g_ffn"], inputs["g_final"])
    wsgT = np.ascontiguousarray(inputs["w_sg"].transpose(0, 3, 1, 2).reshape(NL, 128, 512))
    bsgb = np.ascontiguousarray(np.broadcast_to(inputs["b_sg"].reshape(NL, 1, 512), (NL, 128, 512)))
    return wst, prm, wsgT, bsgb


def kernel(**inputs):
    inputs = {k: np.asarray(v, dtype=np.float32) for k, v in inputs.items()}
    x = inputs["x"]
    nb = x.shape[0]
    wst, prm, wsgT, bsgb = _prep_shared(inputs)
    cst = _consts()
    if "nc" not in _CACHE:
        _CACHE["nc"] = build(order=get_order())[0]
    nc = _CACHE["nc"]
    in_maps = []
    for b in range(nb):
        xTh = np.ascontiguousarray(x[b].T.reshape(8, 128, S).transpose(1, 0, 2).reshape(128, 8 * S))
        in_maps.append({"xT": xTh, "wst": wst, "prm": prm, "cst": cst, "wsgT": wsgT, "bsgb": bsgb})
    res = run_bass_kernel_spmd(nc, in_maps, core_ids=list(range(nb)))
    out = np.empty((nb, S, D), np.float32)
    for b in range(nb):
        yT = np.asarray(res.results[b]["yT"]).reshape(128, 8, S)
        out[b] = yT.transpose(2, 1, 0).reshape(S, D)
    return out
```

```python
import numpy as np
from contextlib import ExitStack
import concourse.bass as bass
import concourse.mybir as mybir
from concourse.bass_utils import run_bass_kernel_spmd

F32 = mybir.dt.float32
BF16 = mybir.dt.bfloat16
AF = mybir.ActivationFunctionType
ALU = mybir.AluOpType
AX = mybir.AxisListType

D = 1024; S = 2048; W = 512; NL = 4; DFF = 4096
T = 1024
NHALF = S // T
TB = 512
NTB = T // TB
NS = 3
SLOT = 4608
NORM_EPS = 1e-6; LN_EPS = 1e-5
FG_POLICY = {"chain0": 2, "chain": 3, "pre": 1, "schain": 3, "omm": 1}
PREFETCH_N1 = True
EW2 = "dve"
FG_STEP = 3

def _block_sizes():
    bs = {"iv": (8, 512), "sgv": (8, 512), "u": (8, 512)}
    for i in range(4):
        bs["hg%d" % i] = (8, 384)
        bs["cv%d" % i] = (8, 384)
    for i in range(8):
        bs["GW%d" % i] = (1, 4608)
        bs["f1_%d" % i] = (8, 512)
        bs["f2_%d" % i] = (32, 128)
    for i in range(2):
        bs["o%d" % i] = (8, 512)
    return bs


BSZ = _block_sizes()
WTOT = sum(k * n for k, n in BSZ.values())
_ORDER = []


def _layout(order):
    off = {}
    o = 0
    for nm in order:
        kc, n = BSZ[nm]
        off[nm] = (o, kc, n)
        o += kc * n
    assert o == WTOT and len(order) == len(BSZ)
    return off


def _tile_cols(w, cols):
    K = w.shape[0]
    return w[:, cols].reshape(K // 128, 128, len(cols)).transpose(1, 0, 2)


def _pack_weights(w_in, w_branch, w_o, w_ff1, w_ff2, order):
    BOFF = _layout(order)
    out = np.empty((NL, 128, WTOT), np.float32)
    ar = np.arange
    for l in range(NL):
        def put(nm, arr):
            o, kc, n = BOFF[nm]
            out[l, :, o:o + kc * n] = arr.reshape(128, kc * n)
        wi = w_in[l]
        put("iv", _tile_cols(wi, ar(1024, 1536)))
        put("sgv", _tile_cols(wi, ar(4096, 4608)))
        put("u", _tile_cols(wi, ar(3584, 4096)))
        for hd in range(4):
            cols = np.concatenate([ar(base + hd * 128, base + hd * 128 + 128) for base in (0, 512, 1536)])
            put("hg%d" % hd, _tile_cols(wi, cols))
        for c in range(4):
            cols = np.concatenate([ar(base + c * 128, base + c * 128 + 128) for base in (2560, 3072, 2048)])
            put("cv%d" % c, _tile_cols(wi, cols))
        for dc in range(8):
            gcols = np.concatenate([ar(4608 + n * 1024 + dc * 128, 4608 + n * 1024 + dc * 128 + 128) for n in range(3)])
            gpart = _tile_cols(wi, gcols).reshape(128, 8 * 384)
            wparts = [_tile_cols(w_branch[l, n], ar(dc * 128, dc * 128 + 128)) for n in range(3)]
            wpart = np.concatenate(wparts, axis=1).reshape(128, 12 * 128)
            put("GW%d" % dc, np.concatenate([gpart, wpart], axis=1))
        for i in range(2):
            put("o%d" % i, _tile_cols(w_o[l], ar(i * 512, i * 512 + 512)))
        for i in range(8):
            put("f1_%d" % i, _tile_cols(w_ff1[l], ar(i * 512, i * 512 + 512)))
        for i in range(8):
            put("f2_%d" % i, _tile_cols(w_ff2[l], ar(i * 128, i * 128 + 128)))
    return out


PC = {}
_c = 0
for _nm, _n in [("gmix", 32), ("gffn", 32), ("gfin", 8), ("lbraw", 16), ("gout", 16), ("wconv", 48),
                ("lng", 16), ("lnb", 16)]:
    PC[_nm] = _c
    _c += _n
NPRM = _c


def _pack_params(g_mix, lower_bounds, g_hgrn_out, w_conv, sg_ln_g, sg_ln_b, g_ffn, g_final):
    p = np.zeros((128, NPRM), np.float32)
    p[:, PC["gmix"]:PC["gmix"] + 32] = g_mix.reshape(NL, 8, 128).transpose(2, 0, 1).reshape(128, 32)
    p[:, PC["gffn"]:PC["gffn"] + 32] = g_ffn.reshape(NL, 8, 128).transpose(2, 0, 1).reshape(128, 32)
    p[:, PC["gfin"]:PC["gfin"] + 8] = g_final.reshape(8, 128).T
    p[:, PC["lbraw"]:PC["lbraw"] + 16] = lower_bounds.reshape(NL, 4, 128).transpose(2, 1, 0).reshape(128, 16)
    p[:, PC["gout"]:PC["gout"] + 16] = g_hgrn_out.reshape(NL, 4, 128).transpose(2, 0, 1).reshape(128, 16)
    p[:, PC["wconv"]:PC["wconv"] + 48] = w_conv.reshape(NL, 3, 4, 128).transpose(3, 0, 1, 2).reshape(128, 48)
    p[:, PC["lng"]:PC["lng"] + 16] = sg_ln_g.reshape(NL, 4, 128).transpose(2, 0, 1).reshape(128, 16)
    p[:, PC["lnb"]:PC["lnb"] + 16] = sg_ln_b.reshape(NL, 4, 128).transpose(2, 0, 1).reshape(128, 16)
    return p


def _consts():
    c = np.zeros((128, 128 * 3 + 512), np.float32)
    c[:, 0:128] = np.eye(128, dtype=np.float32)
    c[:, 128:256] = np.triu(np.ones((128, 128), np.float32))
    c[:, 256:384] = 1.0
    m = np.ones((128, 512), np.float32)
    m[:, ::64] = 0.0
    c[:, 384:896] = m
    return c


class Tk:
    __slots__ = ("name", "w", "r")

    def __init__(self, name):
        self.name = name
        self.w = None
        self.r = {}


class Prog:
    def __init__(self, nc, es):
        self.nc = nc
        self.es = es
        self.E = {"pe": nc.tensor, "act": nc.scalar, "dve": nc.vector, "pool": nc.gpsimd, "sp": nc.sync}
        self.q = {e: [] for e in self.E}
        self.sem = {}
        self.cnt = {}
        self.seen = {e: {} for e in self.E}
        for e in self.E:
            self.newsem(e)

    def newsem(self, name):
        self.sem[name] = self.es.enter_context(self.nc.semaphore("s_" + name))
        self.cnt[name] = 0

    def _waits(self, eng, reads, writes):
        need = {}

        def req(s, v):
            if s == eng and eng == "pe":
                return
            if v > need.get(s, 0):
                need[s] = v
        for t in reads:
            if t.w is not None:
                req(*t.w)
        for t in writes:
            if t.w is not None:
                req(*t.w)
            for s, v in t.r.items():
                req(s, v)
        for s, v in need.items():
            if self.seen[eng].get(s, 0) < v:
                self.seen[eng][s] = v
                self.q[eng].append(("wait", self.sem[s], v))

    def op(self, eng, fn, reads=(), writes=(), inc=True, **kw):
        self._waits(eng, reads, writes)
        nxt = self.cnt[eng] + 1
        self.q[eng].append(("op", fn, kw, self.sem[eng] if inc else None))
        if inc:
            self.cnt[eng] = nxt
        for t in writes:
            t.w = (eng, nxt)
            t.r = {}
        for t in reads:
            if t.r.get(eng, 0) < nxt:
                t.r[eng] = nxt

    def dma(self, qeng, semname, out_ap, in_ap, reads=(), writes=()):
        self._waits(qeng, reads, writes)
        self.cnt[semname] += 16
        v = self.cnt[semname]
        self.q[qeng].append(("dma", out_ap, in_ap, self.sem[semname]))
        for t in writes:
            t.w = (semname, v)
            t.r = {}
        for t in reads:
            t.r[semname] = v
        return v

    def wait_all(self, eng, semname):
        self.q[eng].append(("wait", self.sem[semname], self.cnt[semname]))

    def replay(self, block):
        def run(E, lst):
            for it in lst:
                if it[0] == "wait":
                    E.wait_ge(it[1], it[2])
                elif it[0] == "op":
                    ins = getattr(E, it[1])(**it[2])
                    if it[3] is not None:
                        ins.then_inc(it[3], 1)
                else:
                    E.dma_start(out=it[1], in_=it[2]).then_inc(it[3], 16)
        q = self.q

        @block.tensor
        def _(e):
            run(e, q["pe"])

        @block.scalar
        def _(e):
            run(e, q["act"])

        @block.vector
        def _(e):
            run(e, q["dve"])

        @block.gpsimd
        def _(e):
            run(e, q["pool"])

        @block.sync
        def _(e):
            run(e, q["sp"])


def build(n_layers=NL, final_norm=True, dbg=(), order=None):
    dry = order is None
    req_order = []
    BOFF = None if dry else _layout(order)
    nc = bass.Bass("TRN2", target_bir_lowering=False)
    xin = nc.dram_tensor("xT", [128, 8 * S], F32, kind="ExternalInput").ap()
    wst = nc.dram_tensor("wst", [n_layers, 128, WTOT], F32, kind="ExternalInput").ap()
    prm_d = nc.dram_tensor("prm", [128, NPRM], F32, kind="ExternalInput").ap()
    cst_d = nc.dram_tensor("cst", [128, 896], F32, kind="ExternalInput").ap()
    wsg_d = nc.dram_tensor("wsgT", [n_layers, 128, 512], F32, kind="ExternalInput").ap()
    bsg_d = nc.dram_tensor("bsgb", [n_layers, 128, 512], F32, kind="ExternalInput").ap()
    yout = nc.dram_tensor("yT", [128, 8 * S], F32, kind="ExternalOutput").ap()
    dbg_out = {}

    with ExitStack() as es:
        P = Prog(nc, es)
        for s in ["init", "xld0", "xld1", "st", "wsg", "st0", "st1", "st2"] + ["ws%d" % i for i in range(NS)]:
            P.newsem(s)

        def sb(name, shape, dt):
            return es.enter_context(nc.sbuf_tensor(name, shape, dt))

        xT = sb("xT_sb", [128, 8, S], F32)
        hT = sb("hT", [128, 8, T], BF16)
        wsl = [sb("wsl%d" % i, [128, SLOT], BF16) for i in range(NS)]
        prm = sb("prm_sb", [128, NPRM], F32)
        cst = sb("cst_sb", [128, 768], F32)
        ident = sb("ident", [128, 128], BF16)
        cmask = cst[:, 0:128]
        smask = cst[:, 256:768]
        onesD = sb("onesD", [128, 128], BF16)
        ones128 = sb("ones128", [128, 128], BF16)
        ones1 = sb("ones1", [128, 128], BF16)
        epsT = sb("epsT", [128, 2], F32)
        lbT = sb("lbT", [128, 16, 4], F32)
        lbtmp = sb("lbtmp", [128, 16], F32)
        lbsum = sb("lbsum", [128, 4], F32)
        Sst = sb("Sst", [128, 4, 128], F32)
        halo = sb("halo", [128, 4, 2], F32)
        WmT = sb("WmT", [128, 512], BF16)
        Rg = sb("Rg", [128, 512], F32)
        sqb = [sb("sqb%d" % i, [128, TB], BF16) for i in range(2)]
        KB = 1024
        BIGB = 86 * KB
        big = sb("big", [128, BIGB // 4], F32)
        bigT = {}

        def carve(off, shape, dt):
            esz = 4 if dt == F32 else 2
            n = int(np.prod(shape))
            assert off % 4 == 0 and off + n * esz <= BIGB, (off, shape)
            base = big[:, off // 4:(off + n * esz) // 4]
            if dt != F32:
                base = base.bitcast(dt)
            if len(shape) == 2:
                base = base.rearrange("p (a b) -> p a b", a=shape[0])
            return base, off + n * esz

        zT, o = carve(0, [12, T], BF16)
        vtok, o = carve(24 * KB, [16, 512], BF16)
        gvb, o = carve(40 * KB, [8, 512], BF16)
        gu, o = carve(o, [4, T], BF16)
        cgs, o2 = carve(40 * KB, [1, T], F32)
        zc, o2 = carve(o2, [1, T + 4], F32)
        ycv, o2 = carve(o2, [1, T], F32)
        assert o2 <= 56 * KB
        ut8, _ = carve(56 * KB, [8, 128], F32)
        shist, _ = carve(60 * KB, [8, 128], F32)
        lfb, o = carve(56 * KB, [1, T], F32)
        bdb, o = carve(o, [1, T], F32)
        qsb, o = carve(o, [1, T], BF16)
        sgob, o = carve(o, [1, T], BF16)
        kTb, o = carve(o, [1, T], BF16)
        ktok, o = carve(o, [8, 128], BF16)
        sbf8, o = carve(o, [8, 128], BF16)
        scr, o = carve(o, [4, 512], F32)
        fgx, o = carve(o, [1, 512], F32)
        st1, o = carve(o, [1, 32], F32)
        atm8, o = carve(o, [8, 64], BF16)
        evec, o = carve(o, [4, 16], F32)
        assert o <= BIGB, o
        mergedT, o = carve(24 * KB, [8, T], BF16)
        gtb, o = carve(o, [3, 512], F32)
        macc, o = carve(o, [2, 512], F32)
        mtmp, o = carve(o, [2, 512], F32)
        assert o <= 56 * KB
        aT, o = carve(0, [32, T], BF16)
        rlb, o = carve(64 * KB, [3, 512], F32)
        sq4, o = carve(70 * KB, [4, 512], BF16)

        banks = [es.enter_context(nc.psum_tensor("bank%d" % i, [128, 512], F32)) for i in range(7)]
        ptb = es.enter_context(nc.psum_tensor("ptb", [128, 1024], BF16))
        bankT = [Tk("bank%d" % i) for i in range(7)]
        _b6 = Tk("bank6")
        b6T = [_b6, _b6, _b6, _b6]
        ptT = Tk("ptb")
        rot = [0]

        reserved = set()

        def nbank():
            i = rot[0]
            while i in reserved:
                i = (i + 1) % 7
            rot[0] = (i + 1) % 7
            return banks[i], bankT[i]

        tk = {}

        def K(name):
            if name not in tk:
                tk[name] = Tk(name)
            return tk[name]

        floor = {}

        def B(name):
            if name not in tk:
                t = Tk(name)
                t.r = dict(floor)
                tk[name] = t
            t = tk[name]
            bigT[name] = t
            return t

        def phase_barrier():
            agg = dict(floor)
            for t in bigT.values():
                if t.w is not None:
                    agg[t.w[0]] = max(agg.get(t.w[0], 0), t.w[1])
                for s_, v_ in t.r.items():
                    agg[s_] = max(agg.get(s_, 0), v_)
            floor.clear()
            floor.update(agg)
            for t in bigT.values():
                for s_, v_ in agg.items():
                    if t.r.get(s_, 0) < v_:
                        t.r[s_] = v_

        def inherit(new_names, old_names):
            agg = {}
            for nm in old_names:
                t = B(nm)
                if t.w is not None:
                    agg[t.w[0]] = max(agg.get(t.w[0], 0), t.w[1])
                for s_, v_ in t.r.items():
                    agg[s_] = max(agg.get(s_, 0), v_)
            for nm in new_names:
                t = B(nm)
                for s_, v_ in agg.items():
                    if t.r.get(s_, 0) < v_:
                        t.r[s_] = v_

        P.dma("sp", "init", prm[:], prm_d[:, :], writes=[K("prm")])
        v = P.dma("sp", "init", cst[:], cst_d[:, 128:896], writes=[K("cst")])
        K("prm").w = ("init", v)
        P.newsem("idl")
        P.dma("pool", "idl", ident[:], cst_d[:, 0:128], writes=[K("ident")])
        for h in range(NHALF):
            for kc in range(8):
                v = P.dma("sp", "xld%d" % h, xT[:, kc, h * T:(h + 1) * T], xin[:, kc * S + h * T: kc * S + (h + 1) * T],
                          writes=[K("x%d_%d" % (kc, h))])
            for kc in range(8):
                K("x%d_%d" % (kc, h)).w = ("xld%d" % h, v)

        P.op("dve", "memset", ap=onesD[:], constant=1.0 / 1024.0, writes=[K("onesD")])
        P.op("dve", "memset", ap=ones128[:], constant=1.0 / 128.0, writes=[K("ones128")])
        P.op("dve", "memset", ap=ones1[:], constant=1.0, writes=[K("ones1")])
        P.op("dve", "memset", ap=epsT[:, 0:1], constant=NORM_EPS, writes=[K("eps")])
        P.op("dve", "memset", ap=epsT[:, 1:2], constant=LN_EPS, writes=[K("eps")])
        P.op("dve", "memset", ap=Sst[:], constant=0.0, writes=[K("S%d" % i) for i in range(4)])
        P.op("dve", "memset", ap=halo[:], constant=0.0, writes=[K("halo")])
        P.op("dve", "memset", ap=lbT[:], constant=0.0, writes=[K("lbT")])
        lbr = prm[:, PC["lbraw"]:PC["lbraw"] + 16]
        P.op("act", "activation", out=lbtmp[:], in_=lbr, func=AF.Exp, reads=[K("prm")], writes=[K("lbtmp")])
        lb3 = lbtmp[:].rearrange("p (h l) -> p h l", l=4)
        P.op("dve", "tensor_reduce", out=lbsum[:], in_=lb3, axis=AX.X, op=ALU.add, reads=[K("lbtmp")], writes=[K("lbsum")])
        P.op("dve", "reciprocal", out=lbsum[:], in_=lbsum[:], reads=[K("lbsum")], writes=[K("lbsum")])
        P.op("dve", "tensor_tensor", out=lb3, in0=lb3, in1=lbsum[:].unsqueeze(2).to_broadcast([128, 4, 4]), op=ALU.mult,
             reads=[K("lbtmp"), K("lbsum")], writes=[K("lbtmp")])
        lbv = lbT[:].rearrange("p (h l) c -> p h l c", l=4)
        for l in range(1, 4):
            P.op("dve", "tensor_tensor", out=lbv[:, :, l, 0:1], in0=lbv[:, :, l - 1, 0:1], in1=lb3[:, :, l:l + 1], op=ALU.add,
                 reads=[K("lbT"), K("lbtmp")], writes=[K("lbT")])
        P.op("dve", "tensor_scalar", out=lbT[:, :, 1:2], in0=lbT[:, :, 0:1], scalar1=-1.0, scalar2=None, op0=ALU.add,
             reads=[K("lbT")], writes=[K("lbT")])
        P.op("act", "activation", out=lbT[:, :, 2:3], in_=lbT[:, :, 1:2], func=AF.Ln, scale=-1.0,
             reads=[K("lbT")], writes=[K("lbT")])

        stream = []
        if not dry:
            for l_ in range(n_layers):
                for h_ in range(NHALF):
                    for nm in order:
                        stream.append((l_, nm))
        wstate = {"issued": 0, "consumed": 0}
        slotT = [Tk("slot%d" % i) for i in range(NS)]
        slot_of = {}

        def issue_into(s_):
            j = wstate["issued"]
            if dry or j >= len(stream):
                return
            l_, nm = stream[j]
            o, kc, n = BOFF[nm]
            P.dma("pool", "ws%d" % s_, wsl[s_][:, 0:kc * n], wst[l_, :, o:o + kc * n], writes=[slotT[s_]])
            slot_of[j] = s_
            wstate["issued"] = j + 1

        for s_ in range(NS):
            issue_into(s_)

        class Blk:
            pass

        def next_block(nm):
            j = wstate["consumed"]
            wstate["consumed"] = j + 1
            kc, n = BSZ[nm]
            if dry:
                if len(req_order) < len(BSZ):
                    req_order.append(nm)
                s_ = j % NS
            else:
                assert j < wstate["issued"], ("block not prefetched", nm, j)
                assert stream[j][1] == nm, (stream[j], nm)
                s_ = slot_of.pop(j)
            b = Blk()
            b.slot = s_
            b.flat = wsl[s_]
            if kc > 1:
                b.view = wsl[s_][:, 0:kc * n].rearrange("p (k n) -> p k n", k=kc)
            b.tk = slotT[s_]
            return b

        def finish_block(blk):
            issue_into(blk.slot)

        def subblk(blk, view):
            b = Blk()
            b.view = view
            b.tk = blk.tk
            return b

        def colchunk(blk, c0, rhs_fn, rhs_tk, nk):
            bl = [nbank() for _ in range(NTB)]
            for kc in range(nk):
                for tb in range(NTB):
                    b, bt = bl[tb]
                    P.op("pe", "matmul", out=b[:, :], lhsT=blk.view[:, kc, c0:c0 + 128], rhs=rhs_fn(kc, tb),
                         start=(kc == 0), stop=(kc == nk - 1),
                         reads=[blk.tk] + rhs_tk(kc), writes=[bt], inc=(kc == nk - 1))
            return bl

        def colchunk_tb(blk, c0, rhs_fn, rhs_tk, nk, tb):
            b, bt = nbank()
            for kc in range(nk):
                P.op("pe", "matmul", out=b[:, :], lhsT=blk.view[:, kc, c0:c0 + 128], rhs=rhs_fn(kc, tb),
                     start=(kc == 0), stop=(kc == nk - 1),
                     reads=[blk.tk] + rhs_tk(kc), writes=[bt], inc=(kc == nk - 1))
            return b, bt

        def tsl(tb):
            return slice(tb * TB, (tb + 1) * TB)

        def h_rhs(kc, tb):
            return hT[:, kc, tsl(tb)]

        def h_tk(kc):
            return [K("hT%d" % kc)]

        def rmsnorm_a(h):
            xk = [K("x%d_%d" % (kc, h)) for kc in range(8)]
            bl = [nbank() for _ in range(NTB)]
            for kc in range(8):
                for tb in range(NTB):
                    i = (kc * NTB + tb) % 2
                    sl = slice(h * T + tb * TB, h * T + (tb + 1) * TB)
                    P.op("act", "activation", out=sqb[i][:], in_=xT[:, kc, sl], func=AF.Square,
                         reads=[xk[kc]], writes=[K("sqb%d" % i)])
                    b, bt = bl[tb]
                    P.op("pe", "matmul", out=b[:, :], lhsT=onesD[:], rhs=sqb[i][:], start=(kc == 0), stop=(kc == 7),
                         reads=[K("onesD"), K("sqb%d" % i)], writes=[bt])
            for tb in range(NTB):
                b, bt = bl[tb]
                rs = scr[:, 1 + tb, :]
                P.op("act", "activation", out=rs, in_=b[:, :], func=AF.Ln, bias=epsT[:, 0:1],
                     reads=[bt, K("eps")], writes=[B("scr%d" % (1 + tb))])
                P.op("act", "activation", out=rs, in_=rs, func=AF.Exp, scale=-0.5,
                     reads=[B("scr%d" % (1 + tb))], writes=[B("scr%d" % (1 + tb))])

        def rmsnorm_b(h, emit_out):
            xk = [K("x%d_%d" % (kc, h)) for kc in range(8)]
            for tb in range(NTB):
                for kc in range(8):
                    emit_out(kc, tb, xk[kc])

        def rmsnorm(h, emit_out):
            rmsnorm_a(h)
            rmsnorm_b(h, emit_out)

        def dump(name, ap, tks, shape, dt):
            if name in dbg and name not in dbg_out:
                d = nc.dram_tensor("dbg_" + name, list(shape), dt, kind="ExternalOutput").ap()
                dbg_out[name] = d
                P.dma("sp", "st", d, ap, reads=tks)

        oi = [0]
        fin_pre = {}

        def out_f(kc, tb, xk, h):
            r_ = oi[0] % 3
            oi[0] += 1
            sl = slice(h * T + tb * TB, h * T + (tb + 1) * TB)
            gc = PC["gfin"] + kc
            P.op("dve", "scalar_tensor_tensor", out=rlb[:, r_, :], in0=xT[:, kc, sl], scalar=prm[:, gc:gc + 1],
                 in1=scr[:, 1 + tb, :], op0=ALU.mult, op1=ALU.mult,
                 reads=[xk, K("prm"), B("scr%d" % (1 + tb))], writes=[B("rl%d" % r_)])
            P.dma("sp", "st%d" % r_, yout[:, kc * S + h * T + tb * TB: kc * S + h * T + (tb + 1) * TB], rlb[:, r_, :],
                  reads=[B("rl%d" % r_)])

        n1_pre = {}
        for l in range(n_layers):
            P.dma("sp", "wsg", scr[:, 1, :], wsg_d[l, :, :], writes=[B("scr1")])
            v = P.dma("sp", "wsg", scr[:, 2, :], bsg_d[l, :, :], writes=[B("scr2")])
            B("scr1").w = ("wsg", v)
            P.op("dve", "tensor_tensor", out=WmT[:].rearrange("p (g t) -> p g t", g=4),
                 in0=scr[:, 1, :].rearrange("p (g t) -> p g t", g=4),
                 in1=cmask.unsqueeze(1).to_broadcast([128, 4, 128]), op=ALU.mult,
                 reads=[B("scr1"), K("cst")], writes=[K("WmT")])
            rb, rbt = nbank()
            P.op("pe", "matmul", out=rb[:, :], lhsT=ones1[:], rhs=WmT[:], start=True, stop=True,
                 reads=[K("ones1"), K("WmT")], writes=[rbt])
            for g in range(4):
                gs = slice(g * 128, (g + 1) * 128)
                cb = PC["lnb"] + l * 4 + g
                P.op("dve", "scalar_tensor_tensor", out=Rg[:, gs], in0=rb[:, gs], scalar=prm[:, cb:cb + 1],
                     in1=scr[:, 2, gs], op0=ALU.mult, op1=ALU.add,
                     reads=[rbt, K("prm"), B("scr2")], writes=[K("Rg")])

            for h in range(NHALF):
                t0 = h * T
                phase_barrier()
                if h == 0 and l > 0:
                    P.op("dve", "memset", ap=Sst[:], constant=0.0, writes=[K("S%d" % i) for i in range(4)])
                    P.op("dve", "memset", ap=halo[:], constant=0.0, writes=[K("halo")])
                gmc = PC["gmix"] + l * 8

                def out_h(kc, tb, xk, gbase=gmc, h=h):
                    sl = slice(h * T + tb * TB, h * T + (tb + 1) * TB)
                    P.op("dve", "scalar_tensor_tensor", out=hT[:, kc, tsl(tb)], in0=xT[:, kc, sl],
                         scalar=prm[:, gbase + kc:gbase + kc + 1], in1=scr[:, 1 + tb, :], op0=ALU.mult, op1=ALU.mult,
                         reads=[xk, K("prm"), B("scr%d" % (1 + tb))], writes=[K("hT%d" % kc)])
                if not n1_pre.pop((l, h), False):
                    rmsnorm(h, out_h)
                if l == 0 and h == 0:
                    dump("hT", hT[:], [K("hT%d" % kc) for kc in range(8)], [128, 8, T], BF16)

                fg = []
                fst = {}

                def iv_unit(g4):
                    if g4 == 0:
                        fst["iv"] = next_block("iv")
                    blk = fst["iv"]
                    for c in range(g4 * 4, g4 * 4 + 4):
                        b, bt = nbank()
                        for kc in range(8):
                            P.op("pe", "matmul", out=b[0:64, :], lhsT=hT[:, kc, c * 64:(c + 1) * 64], rhs=blk.view[:, kc, :],
                                 start=(kc == 0), stop=(kc == 7),
                                 reads=[blk.tk, K("hT%d" % kc)], writes=[bt], inc=(kc == 7))
                        P.op("act", "activation", out=vtok[0:64, c, :], in_=b[0:64, :], func=AF.Copy, reads=[bt], writes=[B("vtok%d" % c)])
                    if g4 == 3:
                        finish_block(blk)

                for g4 in range(4):
                    fg.append(lambda g4=g4: iv_unit(g4))

                def sgv_unit(tt, l=l):
                    if tt == 0:
                        fst["sgv"] = next_block("sgv")
                    blk = fst["sgv"]
                    b, bt = nbank()
                    for kc in range(8):
                        P.op("pe", "matmul", out=b[:, :], lhsT=hT[:, kc, tt * 128:(tt + 1) * 128], rhs=blk.view[:, kc, :],
                             start=(kc == 0), stop=(kc == 7),
                             reads=[blk.tk, K("hT%d" % kc)], writes=[bt], inc=(kc == 7))
                    P.op("act", "activation", out=scr[:, 0, :], in_=b[:, :], func=AF.Gelu_apprx_tanh,
                         reads=[bt], writes=[B("scr0")])
                    P.op("dve", "tensor_reduce", out=st1[:, 0, tt:tt + 1], in_=scr[:, 0, :], axis=AX.X, op=ALU.add,
                         reads=[B("scr0")], writes=[B("st1")])
                    P.op(EW2, "tensor_tensor", out=fgx[:, 0, :], in0=scr[:, 0, :], in1=scr[:, 0, :], op=ALU.mult,
                         reads=[B("scr0")], writes=[B("fgx")])
                    P.op("dve", "tensor_reduce", out=st1[:, 0, 8 + tt:9 + tt], in_=fgx[:, 0, :], axis=AX.X, op=ALU.add,
                         reads=[B("fgx")], writes=[B("st1")])
                    P.op("act", "activation", out=gvb[:, tt, :], in_=scr[:, 0, :], func=AF.Copy, reads=[B("scr0")], writes=[B("gv%d" % tt)])
                    if tt == 7:
                        finish_block(blk)
                        P.op("dve", "tensor_scalar", out=st1[:, 0, 16:24], in0=st1[:, 0, 0:8], scalar1=1.0 / 512.0, scalar2=None,
                             op0=ALU.mult, reads=[B("st1")], writes=[B("st1")])
                        P.op("dve", "tensor_tensor", out=st1[:, 0, 0:8], in0=st1[:, 0, 16:24], in1=st1[:, 0, 16:24], op=ALU.mult,
                             reads=[B("st1")], writes=[B("st1")])
                        P.op("dve", "scalar_tensor_tensor", out=st1[:, 0, 24:32], in0=st1[:, 0, 8:16], scalar=1.0 / 512.0,
                             in1=st1[:, 0, 0:8], op0=ALU.mult, op1=ALU.subtract, reads=[B("st1")], writes=[B("st1")])
                        P.op("act", "activation", out=st1[:, 0, 24:32], in_=st1[:, 0, 24:32], func=AF.Ln, bias=epsT[:, 1:2],
                             reads=[B("st1"), K("eps")], writes=[B("st1")])
                        P.op("act", "activation", out=st1[:, 0, 24:32], in_=st1[:, 0, 24:32], func=AF.Exp, scale=-0.5,
                             reads=[B("st1")], writes=[B("st1")])
                        P.op("dve", "scalar_tensor_tensor", out=st1[:, 0, 0:8], in0=st1[:, 0, 16:24], scalar=-1.0,
                             in1=st1[:, 0, 24:32], op0=ALU.mult, op1=ALU.mult, reads=[B("st1")], writes=[B("st1")])
                        for t2 in range(8):
                            P.op("act", "activation", out=gvb[:, t2, :], in_=gvb[:, t2, :], func=AF.Identity,
                                 scale=st1[:, 0, 24 + t2:25 + t2], bias=st1[:, 0, t2:t2 + 1],
                                 reads=[B("gv%d" % t2), B("st1")], writes=[B("gv%d" % t2)])

                def u_unit(g, tb):
                    if g == 0 and tb == 0:
                        fst["u"] = next_block("u")
                    blk = fst["u"]
                    b, bt = colchunk_tb(blk, g * 128, h_rhs, h_tk, 8, tb)
                    P.op("act", "activation", out=gu[:, g, tsl(tb)], in_=b[:, :], func=AF.Gelu_apprx_tanh,
                         reads=[bt], writes=[B("gu%d_%d" % (g, tb))])
                    if g == 3 and tb == NTB - 1:
                        finish_block(blk)

                def m1t_unit(g, tb, l=l):
                    gs = slice(g * 128, (g + 1) * 128)
                    cg_ = PC["lng"] + l * 4 + g
                    b, bt = nbank()
                    for j in range(4):
                        tt = tb * 4 + j
                        P.op("pe", "matmul", out=b[:, j * 128:(j + 1) * 128], lhsT=gvb[:, tt, gs], rhs=WmT[:, gs],
                             start=True, stop=True, reads=[B("gv%d" % tt), K("WmT")], writes=[bt], inc=(j == 3))
                    sv_ap = scr[:, 0, :] if tb == 0 else fgx[:, 0, :]
                    SV = B("scr0") if tb == 0 else B("fgx")
                    P.op("dve", "scalar_tensor_tensor", out=sv_ap.rearrange("p (j t) -> p j t", j=4),
                         in0=b[:, :].rearrange("p (j t) -> p j t", j=4), scalar=prm[:, cg_:cg_ + 1],
                         in1=Rg[:, gs].unsqueeze(1).to_broadcast([128, 4, 128]), op0=ALU.mult, op1=ALU.add,
                         reads=[bt, K("prm"), K("Rg")], writes=[SV])
                    P.op(EW2, "tensor_tensor", out=zT[:, 8 + g, tsl(tb)], in0=sv_ap, in1=gu[:, g, tsl(tb)],
                         op=ALU.mult, reads=[SV, B("gu%d_%d" % (g, tb))], writes=[B("zT%d" % (8 + g))])

                def conv_unit(cc, which, tb, l=l):
                    def wc(k):
                        c_ = PC["wconv"] + l * 12 + k * 4 + cc
                        return prm[:, c_:c_ + 1]
                    if which == 0 and tb == 0:
                        fst["cv"] = next_block("cv%d" % cc)
                        if cc == 0:
                            inherit(["cgs0", "cgs1", "zc", "ycv"],
                                    ["gv%d" % i for i in range(8)] + ["gu%d_%d" % (g_, t_) for g_ in range(4) for t_ in range(NTB)])
                    blk = fst["cv"]
                    b, bt = colchunk_tb(blk, which * 128, h_rhs, h_tk, 8, tb)
                    if which == 0:
                        P.op("act", "activation", out=cgs[:, 0, tsl(tb)], in_=b[:, :], func=AF.Copy, reads=[bt], writes=[B("cgs%d" % tb)])
                    elif which == 1:
                        if tb == 0:
                            P.op("dve", "tensor_copy", out=zc[:, 0, 0:2], in_=halo[:, cc, :], reads=[K("halo")], writes=[B("zc")])
                        P.op("dve", "tensor_tensor", out=zc[:, 0, 2 + tb * TB:2 + (tb + 1) * TB], in0=b[:, :], in1=cgs[:, 0, tsl(tb)],
                             op=ALU.mult, reads=[bt, B("cgs%d" % tb)], writes=[B("zc")])
                        if tb == NTB - 1:
                            P.op("dve", "tensor_copy", out=halo[:, cc, :], in_=zc[:, 0, T:T + 2], reads=[B("zc")], writes=[K("halo")])
                            P.op("act", "activation", out=ycv[:, 0, :], in_=zc[:, 0, 2:T + 2], func=AF.Copy, scale=wc(2),
                                 reads=[B("zc"), K("prm")], writes=[B("ycv")])
                            P.op("dve", "scalar_tensor_tensor", out=ycv[:, 0, :], in0=zc[:, 0, 1:T + 1], scalar=wc(1), in1=ycv[:, 0, :],
                                 op0=ALU.mult, op1=ALU.add, reads=[B("zc"), K("prm"), B("ycv")], writes=[B("ycv")])
                            P.op("dve", "scalar_tensor_tensor", out=ycv[:, 0, :], in0=zc[:, 0, 0:T], scalar=wc(0), in1=ycv[:, 0, :],
                                 op0=ALU.mult, op1=ALU.add, reads=[B("zc"), K("prm"), B("ycv")], writes=[B("ycv")])
                    else:
                        if tb == NTB - 1:
                            finish_block(blk)
                        P.op("dve", "tensor_tensor", out=zT[:, 4 + cc, tsl(tb)], in0=b[:, :], in1=ycv[:, 0, tsl(tb)], op=ALU.mult,
                             reads=[bt, B("ycv")], writes=[B("zT%d" % (4 + cc))])

                for tt in range(8):
                    fg.append(lambda tt=tt: sgv_unit(tt))
                for g in range(4):
                    for tb in range(NTB):
                        fg.append(lambda g=g, tb=tb: u_unit(g, tb))
                for g in range(4):
                    for tb in range(NTB):
                        fg.append(lambda g=g, tb=tb: m1t_unit(g, tb))
                for cc in range(4):
                    for which in range(3):
                        for tb in range(NTB):
                            fg.append(lambda cc=cc, which=which, tb=tb: conv_unit(cc, which, tb))

                def hgrn_gen(l=l, h=h):
                    for hd in range(4):
                        lbi = hd * 4 + l
                        nom = lbT[:, lbi, 1:2]
                        lnom = lbT[:, lbi, 2:3]
                        sT = K("S%d" % hd)
                        hs = slice(hd * 128, (hd + 1) * 128)
                        inherit(["lf0", "lf1"], ["ut8"])
                        inherit(["bd0", "bd1"], ["shist"])
                        blk = next_block("hg%d" % hd)
                        bl = colchunk(blk, 128, h_rhs, h_tk, 8)
                        for tb in range(NTB):
                            b, bt = bl[tb]
                            P.op("act", "activation", out=lfb[:, 0, tsl(tb)], in_=b[:, :], func=AF.Sigmoid, scale=-1.0,
                                 reads=[bt], writes=[B("lf%d" % tb)])
                        yield "proj"
                        Xs = [lfb[:, 0, tsl(tb)] for tb in range(NTB)]
                        Ys = [bdb[:, 0, tsl(tb)] for tb in range(NTB)]
                        Zs = [scr[:, 2 + tb, :] for tb in range(NTB)]
                        XT = [B("lf%d" % tb) for tb in range(NTB)]
                        YT = [B("bd%d" % tb) for tb in range(NTB)]
                        ZT = [B("scr%d" % (2 + tb)) for tb in range(NTB)]
                        cks = [slice(tb * 8, (tb + 1) * 8) for tb in range(NTB)]
                        for tb in range(NTB):
                            P.op("act", "activation", out=Ys[tb], in_=Xs[tb], func=AF.Ln, scale=nom, bias=1.0,
                                 reads=[XT[tb], K("lbT")], writes=[YT[tb]])
                        z3 = [z.rearrange("p (c t) -> p c t", t=64) for z in Zs]
                        for tb in range(NTB):
                            P.op("dve", "tensor_tensor_scan", out=Zs[tb], data0=smask, data1=Ys[tb], initial=0.0, op0=ALU.mult, op1=ALU.add,
                                 reads=[YT[tb], K("cst")], writes=[ZT[tb]])
                        for tb in range(NTB):
                            P.op("dve", "tensor_tensor", out=evec[:, 0, cks[tb]].unsqueeze(2), in0=z3[tb][:, :, 63:64], in1=z3[tb][:, :, 31:32],
                                 op=ALU.subtract, reads=[ZT[tb]], writes=[B("ev0")])
                            P.op("dve", "tensor_tensor", out=Ys[tb].rearrange("p (c t) -> p c t", t=64), in0=z3[tb],
                                 in1=z3[tb][:, :, 31:32].to_broadcast([128, 8, 64]), op=ALU.subtract, reads=[ZT[tb], YT[tb]], writes=[YT[tb]])
                        for tb in range(NTB):
                            P.op("act", "activation", out=evec[:, 1, cks[tb]].unsqueeze(2), in_=z3[tb][:, :, 31:32], func=AF.Exp,
                                 reads=[ZT[tb]], writes=[B("ev1")])
                            P.op("act", "activation", out=evec[:, 2, cks[tb]].unsqueeze(2), in_=z3[tb][:, :, 63:64], func=AF.Exp,
                                 reads=[ZT[tb]], writes=[B("ev2")])
                        for tb in range(NTB):
                            P.op("act", "activation", out=Zs[tb], in_=Ys[tb], func=AF.Exp, scale=-1.0, bias=lnom,
                                 reads=[YT[tb], K("lbT")], writes=[ZT[tb]])
                            P.op(EW2, "tensor_tensor", out=kTb[:, 0, tsl(tb)], in0=Xs[tb], in1=Zs[tb], op=ALU.mult,
                                 reads=[XT[tb], ZT[tb]], writes=[B("kT%d" % tb)])
                        bl = colchunk(blk, 0, h_rhs, h_tk, 8)
                        for tb in range(NTB):
                            b, bt = bl[tb]
                            P.op("act", "activation", out=scr[:, 1, :], in_=b[:, :], func=AF.Sigmoid, reads=[bt], writes=[B("scr1")])
                            P.op("dve", "tensor_tensor", out=qsb[:, 0, tsl(tb)], in0=b[:, :], in1=scr[:, 1, :], op=ALU.mult,
                                 reads=[bt, B("scr1")], writes=[B("qs%d" % tb)])
                        yield "proj"
                        bl = colchunk(blk, 256, h_rhs, h_tk, 8)
                        finish_block(blk)
                        for tb in range(NTB):
                            b, bt = bl[tb]
                            P.op("act", "activation", out=sgob[:, 0, tsl(tb)], in_=b[:, :], func=AF.Sigmoid,
                                 reads=[bt], writes=[B("sgo%d" % tb)])
                        yield "proj"
                        for tb in range(NTB):
                            P.op("act", "activation", out=scr[:, 1, :], in_=Ys[tb], func=AF.Exp, reads=[YT[tb]], writes=[B("scr1")])
                            P.op(EW2, "tensor_tensor", out=qsb[:, 0, tsl(tb)], in0=qsb[:, 0, tsl(tb)], in1=scr[:, 1, :], op=ALU.mult,
                                 reads=[B("qs%d" % tb), B("scr1")], writes=[B("qs%d" % tb)])
                        P.op("act", "activation", out=evec[:, 3, :], in_=evec[:, 0, :], func=AF.Exp,
                             reads=[B("ev0")], writes=[B("ev3")])
                        yield ("chain0" if hd == 0 else "chain")
                        inherit(["ut8"], ["lf0", "lf1"])
                        inherit(["shist"], ["bd0", "bd1"])
                        if l == 0 and h == 0 and hd == 0:
                            dump("qt", qsb[:, 0, :], [B("qs0"), B("qs1")], [128, T], BF16)
                            dump("kt", kTb[:, 0, :], [B("kT0"), B("kT1")], [128, T], BF16)
                        for tb in range(NTB):
                            ck = slice(tb * 8, (tb + 1) * 8)
                            for c8 in range(8):
                                c = tb * 8 + c8
                                P.op("pe", "transpose", out=ptb[0:64, c8 * 128:(c8 + 1) * 128], in_=kTb[:, 0, c * 64:(c + 1) * 64],
                                     identity=ident[:], reads=[B("kT%d" % tb), K("ident")], writes=[ptT], inc=(c8 == 7))
                            ab, abt = nbank()
                            for c8 in range(8):
                                cs = slice((tb * 8 + c8) * 64, (tb * 8 + c8 + 1) * 64)
                                P.op("pe", "matmul", out=ab[0:64, c8 * 64:(c8 + 1) * 64], lhsT=kTb[:, 0, cs], rhs=qsb[:, 0, cs],
                                     start=True, stop=True, reads=[B("kT%d" % tb), B("qs%d" % tb)], writes=[abt], inc=(c8 == 7))
                            P.op("dve", "tensor_tensor", out=atm8[0:64, :, :], in0=ab[0:64, :].rearrange("p (c t) -> p c t", c=8),
                                 in1=cmask[0:64, 0:64].unsqueeze(1).to_broadcast([64, 8, 64]), op=ALU.mult,
                                 reads=[abt, K("cst")], writes=[B("atm8")])
                            P.op("act", "activation", out=ktok[0:64, 0:8, :],
                                 in_=ptb[0:64, :].rearrange("p (c k) -> p c k", c=8), func=AF.Copy, reads=[ptT], writes=[B("ktok")])
                            yield "pre"
                            for half8 in range(2):
                                ub, ubt = nbank()
                                for j in range(4):
                                    c8 = half8 * 4 + j
                                    c = tb * 8 + c8
                                    P.op("pe", "matmul", out=ub[:, j * 128:(j + 1) * 128], lhsT=ktok[0:64, c8, :],
                                         rhs=vtok[0:64, c, hs], start=True, stop=True,
                                         reads=[B("ktok"), B("vtok%d" % c)], writes=[ubt], inc=(j == 3))
                                c0 = tb * 8 + half8 * 4
                                P.op("dve", "tensor_tensor", out=ut8[:, half8 * 4:half8 * 4 + 4, :],
                                     in0=ub[:, :].rearrange("p (c v) -> p c v", c=4),
                                     in1=evec[:, 3, c0:c0 + 4].unsqueeze(2).to_broadcast([128, 4, 128]), op=ALU.mult,
                                     reads=[ubt, B("ev3")], writes=[B("ut8")])
                            P.op("dve", "tensor_copy", out=shist[:, 0, :], in_=Sst[:, hd, :], reads=[sT], writes=[B("shist")])
                            for c8 in range(8):
                                c = tb * 8 + c8
                                if c8 < 7:
                                    P.op("dve", "scalar_tensor_tensor", out=shist[:, c8 + 1, :], in0=shist[:, c8, :],
                                         scalar=evec[:, 2, c:c + 1], in1=ut8[:, c8, :], op0=ALU.mult, op1=ALU.add,
                                         reads=[B("shist"), B("ut8"), B("ev2")], writes=[B("shist")])
                                else:
                                    P.op("dve", "scalar_tensor_tensor", out=Sst[:, hd, :], in0=shist[:, c8, :],
                                         scalar=evec[:, 2, c:c + 1], in1=ut8[:, c8, :], op0=ALU.mult, op1=ALU.add,
                                         reads=[B("shist"), B("ut8"), B("ev2")], writes=[sT])
                            P.op("dve", "tensor_tensor", out=sbf8[:, :, :], in0=shist[:, :, :],
                                 in1=evec[:, 1, ck].unsqueeze(2).to_broadcast([128, 8, 128]), op=ALU.mult,
                                 reads=[B("shist"), B("ev1")], writes=[B("sbf8")])
                            yield "schain"
                            ob, obt = nbank()
                            for c8 in range(8):
                                c = tb * 8 + c8
                                cs = slice(c * 64, (c + 1) * 64)
                                P.op("pe", "matmul", out=ob[:, c8 * 64:(c8 + 1) * 64], lhsT=vtok[0:64, c, hs], rhs=atm8[0:64, c8, :],
                                     start=True, stop=False, reads=[B("vtok%d" % c), B("atm8")], writes=[obt], inc=False)
                                P.op("pe", "matmul", out=ob[:, c8 * 64:(c8 + 1) * 64], lhsT=sbf8[:, c8, :], rhs=qsb[:, 0, cs],
                                     start=False, stop=True, reads=[B("sbf8"), B("qs%d" % tb)], writes=[obt], inc=(c8 == 7))
                            P.op("act", "activation", out=sqb[0][:], in_=ob[:, :], func=AF.Square, reads=[obt], writes=[K("sqb0")])
                            yield "omm"
                            mb, mbt = nbank()
                            P.op("pe", "matmul", out=mb[:, :], lhsT=ones128[:], rhs=sqb[0][:], start=True, stop=True,
                                 reads=[K("ones128"), K("sqb0")], writes=[mbt])
                            P.op("act", "activation", out=scr[:, 1, :], in_=mb[:, :], func=AF.Ln, bias=epsT[:, 0:1],
                                 reads=[mbt, K("eps")], writes=[B("scr1")])
                            P.op("act", "activation", out=scr[:, 1, :], in_=scr[:, 1, :], func=AF.Exp, scale=-0.5,
                                 reads=[B("scr1")], writes=[B("scr1")])
                            cgo = PC["gout"] + l * 4 + hd
                            P.op("dve", "scalar_tensor_tensor", out=scr[:, 3, :], in0=ob[:, :], scalar=prm[:, cgo:cgo + 1],
                                 in1=scr[:, 1, :], op0=ALU.mult, op1=ALU.mult, reads=[obt, K("prm"), B("scr1")], writes=[B("scr3")])
                            P.op("dve", "tensor_tensor", out=zT[:, hd, tsl(tb)], in0=scr[:, 3, :], in1=sgob[:, 0, tsl(tb)], op=ALU.mult,
                                 reads=[B("scr3"), B("sgo%d" % tb)], writes=[B("zT%d" % hd)])

                fgi = 0
                stepn = 0
                for kind in hgrn_gen():
                    nfg = 0
                    nfg = FG_POLICY.get(kind, 0)
                    for _ in range(nfg):
                        if fgi < len(fg):
                            fg[fgi]()
                            fgi += 1
                while fgi < len(fg):
                    fg[fgi]()
                    fgi += 1
                if l == 0 and h == 0:
                    dump("zT", zT[:], [B("zT%d" % i) for i in range(12)], [128, 12, T], BF16)

                phase_barrier()
                gtc = 0
                for dc in range(8):
                    blk = next_block("GW%d" % dc)
                    gview = subblk(blk, blk.flat[:, 0:3072].rearrange("p (k n) -> p k n", k=8))
                    wview = blk.flat[:, 3072:4608].rearrange("p (k n) -> p k n", k=12)
                    for n in range(3):
                        gl = colchunk(gview, n * 128, h_rhs, h_tk, 8)

                        def z_rhs(kc, tb, n=n):
                            return zT[:, n * 4 + kc, tsl(tb)]

                        def z_tk(kc, n=n):
                            return [B("zT%d" % (n * 4 + kc))]
                        yl = colchunk(subblk(blk, wview[:, n * 4:(n + 1) * 4, :]), 0, z_rhs, z_tk, 4)
                        for tb in range(NTB):
                            g_b, g_bt = gl[tb]
                            y_b, y_bt = yl[tb]
                            gi_ = gtc % 3
                            gtc += 1
                            GT = B("gt%d" % gi_)
                            P.op("act", "activation", out=gtb[:, gi_, :], in_=g_b[:, :], func=AF.Sigmoid, reads=[g_bt], writes=[GT])
                            if n == 0:
                                P.op("dve", "tensor_tensor", out=macc[:, tb, :], in0=y_b[:, :], in1=gtb[:, gi_, :], op=ALU.mult,
                                     reads=[y_bt, GT], writes=[B("macc%d" % tb)])
                            else:
                                P.op("dve", "tensor_tensor", out=mtmp[:, tb, :], in0=y_b[:, :], in1=gtb[:, gi_, :], op=ALU.mult,
                                     reads=[y_bt, GT], writes=[B("mtmp%d" % tb)])
                                if n == 1:
                                    P.op("dve", "tensor_tensor", out=macc[:, tb, :], in0=macc[:, tb, :], in1=mtmp[:, tb, :], op=ALU.add,
                                         reads=[B("macc%d" % tb), B("mtmp%d" % tb)], writes=[B("macc%d" % tb)])
                                else:
                                    P.op("dve", "tensor_tensor", out=mergedT[:, dc, tsl(tb)], in0=macc[:, tb, :], in1=mtmp[:, tb, :],
                                         op=ALU.add, reads=[B("macc%d" % tb), B("mtmp%d" % tb)], writes=[B("mg%d" % dc)])
                    finish_block(blk)

                def m_rhs(kc, tb):
                    return mergedT[:, kc, tsl(tb)]

                def m_tk(kc):
                    return [B("mg%d" % kc)]
                gfc = PC["gffn"] + l * 8
                nb_ = [nbank() for _ in range(NTB)]
                for b_, bt_ in nb_:
                    reserved.add(banks.index(b_))
                sqi = 0
                pend = []

                def ms_mm(oc, tb, i):
                    P.op("pe", "matmul", out=nb_[tb][0][:, :], lhsT=onesD[:], rhs=sq4[:, i, :], start=(oc == 0), stop=(oc == 7),
                         reads=[K("onesD"), B("sq4_%d" % i)], writes=[nb_[tb][1]])
                for ob_ in range(2):
                    blk = next_block("o%d" % ob_)
                    for j in range(4):
                        oc = ob_ * 4 + j
                        bl = colchunk(blk, j * 128, m_rhs, m_tk, 8)
                        XK = K("x%d_%d" % (oc, h))
                        for tb in range(NTB):
                            b, bt = bl[tb]
                            sl = slice(t0 + tb * TB, t0 + (tb + 1) * TB)
                            P.op("dve", "tensor_tensor", out=xT[:, oc, sl], in0=b[:, :], in1=xT[:, oc, sl], op=ALU.add,
                                 reads=[bt, XK], writes=[XK])
                        for tb in range(NTB):
                            sl = slice(t0 + tb * TB, t0 + (tb + 1) * TB)
                            P.op("dve", "tensor_scalar", out=hT[:, oc, tsl(tb)], in0=xT[:, oc, sl], scalar1=prm[:, gfc + oc:gfc + oc + 1],
                                 scalar2=None, op0=ALU.mult, reads=[XK, K("prm")], writes=[K("hT%d" % oc)])
                            i = sqi % 4
                            sqi += 1
                            P.op("act", "activation", out=sq4[:, i, :], in_=xT[:, oc, sl], func=AF.Square,
                                 reads=[XK], writes=[B("sq4_%d" % i)])
                            pend.append((oc, tb, i))
                        while len(pend) > NTB:
                            ms_mm(*pend.pop(0))
                    finish_block(blk)

                def finish_r2():
                    while pend:
                        ms_mm(*pend.pop(0))
                    r2_tail()

                def r2_tail():
                  for tb in range(NTB):
                      b_, bt_ = nb_[tb]
                      rs = scr[:, 1 + tb, :]
                      P.op("act", "activation", out=rs, in_=b_[:, :], func=AF.Ln, bias=epsT[:, 0:1],
                           reads=[bt_, K("eps")], writes=[B("scr%d" % (1 + tb))])
                      P.op("act", "activation", out=rs, in_=rs, func=AF.Exp, scale=-1.0,
                           reads=[B("scr%d" % (1 + tb))], writes=[B("scr%d" % (1 + tb))])
                      reserved.discard(banks.index(b_))
                r2_state = {"done": False}
                if l == 0 and h == 0:
                    dump("x1", xT[:, :, 0:T], [K("x%d_0" % i) for i in range(8)], [128, 8, T], F32)
                phase_barrier()
                phase_barrier()
                ri = 0
                for fb in range(8):
                    blk = next_block("f1_%d" % fb)
                    for j in range(4):
                        jc = fb * 4 + j
                        bl = colchunk(blk, j * 128, h_rhs, h_tk, 8)
                        if not r2_state["done"]:
                            r2_state["done"] = True
                            finish_r2()
                        for tb in range(NTB):
                            b, bt = bl[tb]
                            r_ = ri % 3
                            ri += 1
                            P.op("act", "activation", out=rlb[:, r_, :], in_=b[:, :], func=AF.Relu, reads=[bt], writes=[B("rl%d" % r_)])
                            P.op("act", "activation", out=rlb[:, r_, :], in_=rlb[:, r_, :], func=AF.Square,
                                 reads=[B("rl%d" % r_)], writes=[B("rl%d" % r_)])
                            P.op("dve", "tensor_tensor", out=aT[:, jc, tsl(tb)], in0=rlb[:, r_, :], in1=scr[:, 1 + tb, :], op=ALU.mult,
                                 reads=[B("rl%d" % r_), B("scr%d" % (1 + tb))], writes=[B("aT%d" % jc)])
                    finish_block(blk)

                def a_rhs(kc, tb):
                    return aT[:, kc, tsl(tb)]

                def a_tk(kc):
                    return [B("aT%d" % kc)]
                for oc in range(8):
                    blk = next_block("f2_%d" % oc)
                    bl = colchunk(blk, 0, a_rhs, a_tk, 32)
                    for tb in range(NTB):
                        b, bt = bl[tb]
                        sl = slice(t0 + tb * TB, t0 + (tb + 1) * TB)
                        XK = K("x%d_%d" % (oc, h))
                        P.op("dve", "tensor_tensor", out=xT[:, oc, sl], in0=b[:, :], in1=xT[:, oc, sl], op=ALU.add,
                             reads=[bt, XK], writes=[XK])
                    finish_block(blk)
                    nxt = (l, h + 1) if h + 1 < NHALF else ((l + 1, 0) if l + 1 < n_layers else None)
                    if PREFETCH_N1 and nxt is None and final_norm and NHALF == 2 and h == 1:
                        if oc == 1:
                            rmsnorm_a(0)
                        elif oc == 2:
                            rmsnorm_b(0, lambda kc, tb, xk: out_f(kc, tb, xk, 0))
                            fin_pre[0] = True
                    if PREFETCH_N1 and nxt is not None:
                        if oc == 1:
                            rmsnorm_a(nxt[1])
                        elif oc == 2:
                            gb_ = PC["gmix"] + nxt[0] * 8
                            rmsnorm_b(nxt[1], lambda kc, tb, xk, gb_=gb_, nh=nxt[1]: out_h(kc, tb, xk, gb_, nh))
                            n1_pre[nxt] = True

        for h in range(NHALF):
            if final_norm:
                if not fin_pre.get(h):
                    rmsnorm(h, lambda kc, tb, xk, h=h: out_f(kc, tb, xk, h))
            else:
                for kc in range(8):
                    P.dma("sp", "st", yout[:, kc * S + h * T: kc * S + (h + 1) * T], xT[:, kc, h * T:(h + 1) * T],
                          reads=[K("x%d_%d" % (kc, h))])
        for s_ in ["st", "st0", "st1", "st2"]:
            if P.cnt[s_] > 0:
                P.wait_all("sp", s_)
        if dry:
            return None, list(req_order)
        block = es.enter_context(nc.Block())
        P.replay(block)
    return nc, dbg_out


def get_order():
    if not _ORDER:
        _ORDER.extend(build(n_layers=1, final_norm=False, order=None)[1])
    return list(_ORDER)


_CACHE = {}


def _prep_shared(inputs):
    wst = _pack_weights(inputs["w_in"], inputs["w_branch"], inputs["w_o"], inputs["w_ff1"], inputs["w_ff2"], get_order())
    prm = _pack_params(inputs["g_mix"], inputs["lower_bounds"], inputs["g_hgrn_out"], inputs["w_conv"],
                       inputs["sg_ln_g"], inputs["sg_ln_b"], inputs["g_ffn"], inputs["g_final"])
    wsgT = np.ascontiguousarray(inputs["w_sg"].transpose(0, 3, 1, 2).reshape(NL, 128, 512))
    bsgb = np.ascontiguousarray(np.broadcast_to(inputs["b_sg"].reshape(NL, 1, 512), (NL, 128, 512)))
    return wst, prm, wsgT, bsgb


def kernel(**inputs):
    inputs = {k: np.asarray(v, dtype=np.float32) for k, v in inputs.items()}
    x = inputs["x"]
    nb = x.shape[0]
    wst, prm, wsgT, bsgb = _prep_shared(inputs)
    cst = _consts()
    if "nc" not in _CACHE:
        _CACHE["nc"] = build(order=get_order())[0]
    nc = _CACHE["nc"]
    in_maps = []
    for b in range(nb):
        xTh = np.ascontiguousarray(x[b].T.reshape(8, 128, S).transpose(1, 0, 2).reshape(128, 8 * S))
        in_maps.append({"xT": xTh, "wst": wst, "prm": prm, "cst": cst, "wsgT": wsgT, "bsgb": bsgb})
    res = run_bass_kernel_spmd(nc, in_maps, core_ids=list(range(nb)))
    out = np.empty((nb, S, D), np.float32)
    for b in range(nb):
        yT = np.asarray(res.results[b]["yT"]).reshape(128, 8, S)
        out[b] = yT.transpose(2, 1, 0).reshape(S, D)
    return out
```
